# Optimizing a Trainium2 kernel written in Bass

```python
import math
import jax, jax.numpy as jnp
from jax import lax
import numpy as np

D_MODEL = 1024
BATCH = 8
SEQ = 2048
DEPTH = 2

N_A = max(1, DEPTH // 2)
N_B = DEPTH - N_A
D_CONV = D_MODEL
CONV_K = 3
N_HEADS = 16
HEAD_DIM = D_MODEL // N_HEADS
D_ATTN = N_HEADS * HEAD_DIM
Q_BLOCK = 128
EPS = 1e-6

kernel_name = "yoco_shortconv_stickbreaking_adaln"


def rmsnorm(x, g):
    xf = x.astype(jnp.float32)
    y = xf * lax.rsqrt(jnp.mean(xf * xf, axis=-1, keepdims=True) + EPS)
    return (y * g.astype(jnp.float32)).astype(x.dtype)


def modulation(c, w, b, n):
    m = jax.nn.silu(c) @ w + b
    return jnp.split(m, n, axis=-1)


def modulated_norm(x, g, shift, scale):
    return rmsnorm(x, g) * (1.0 + scale[:, None, :]) + shift[:, None, :]


def causal_dwconv(u, w):
    return lax.conv_general_dilated(
        u, w[:, None, :].astype(u.dtype), window_strides=(1,),
        padding=[(CONV_K - 1, 0)], dimension_numbers=("NWC", "WIO", "NWC"),
        feature_group_count=u.shape[-1])


def short_conv_layer(x, c, mod_w, mod_b, norm_g, w_in, conv_w, w_out):
    shift, scale, gate = modulation(c, mod_w, mod_b, 3)
    h = modulated_norm(x, norm_g, shift, scale)
    proj = h @ w_in
    b_gate, c_gate, u, z = jnp.split(proj, 4, axis=-1)
    y = b_gate * causal_dwconv(c_gate * u, conv_w)
    y = y * jax.nn.silu(z)
    return x + gate[:, None, :] * (y @ w_out)


def stick_breaking_attention(q, k, v):
    seq = q.shape[2]
    inv_sqrt_d = 1.0 / math.sqrt(q.shape[-1])
    outs = []
    for i in range(seq // Q_BLOCK):
        q0 = i * Q_BLOCK
        kend = q0 + Q_BLOCK
        qb = q[:, :, q0:kend].astype(jnp.float32)
        kb = k[:, :, :kend].astype(jnp.float32)
        vb = v[:, :, :kend]
        z = jnp.einsum('bhqd,bhkd->bhqk', qb, kb) * inv_sqrt_d
        qpos = q0 + jnp.arange(Q_BLOCK)[:, None]
        kpos = jnp.arange(kend)[None, :]
        mask = kpos < qpos
        log_beta = jax.nn.log_sigmoid(z)
        log_1m_beta = jnp.where(mask, log_beta - z, 0.0)
        suffix = lax.cumsum(log_1m_beta, axis=3, reverse=True) - log_1m_beta
        a = jnp.where(mask, jnp.exp(log_beta + suffix), 0.0)
        outs.append(jnp.einsum('bhqk,bhkd->bhqd', a.astype(vb.dtype), vb))
    return jnp.concatenate(outs, axis=2)


def split_heads(t):
    b, s, _ = t.shape
    return t.reshape(b, s, N_HEADS, HEAD_DIM).transpose(0, 2, 1, 3)


def stick_breaking_layer(x, c, k, v, mod_w, mod_b, norm_g, w_qz, w_out):
    shift, scale, gate = modulation(c, mod_w, mod_b, 3)
    h = modulated_norm(x, norm_g, shift, scale)
    proj = h @ w_qz
    q, z = jnp.split(proj, 2, axis=-1)
    o = stick_breaking_attention(split_heads(q), k, v)
    b, _, s, _ = o.shape
    o = o.transpose(0, 2, 1, 3).reshape(b, s, D_ATTN)
    o = o * jax.nn.silu(z)
    return x + gate[:, None, :] * (o @ w_out)


def setup_inputs(seed: int = 0) -> dict:
    key = jax.random.key(seed)
    ks = jax.random.split(key, 20)
    D = D_MODEL
    nrm = lambda k, shape, s: jax.random.normal(k, shape, jnp.float32) * s
    return {
        "x": nrm(ks[0], (BATCH, SEQ, D), 1.0),
        "c": nrm(ks[1], (BATCH, D), 1.0),
        "a_mod_w": nrm(ks[2], (N_A, D, 3 * D), 0.02),
        "a_mod_b": nrm(ks[3], (N_A, 3 * D), 0.01),
        "a_norm_g": 1.0 + nrm(ks[4], (N_A, D), 0.02),
        "a_w_in": nrm(ks[5], (N_A, D, 4 * D_CONV), D ** -0.5),
        "a_conv_w": nrm(ks[6], (N_A, CONV_K, D_CONV), CONV_K ** -0.5),
        "a_w_out": nrm(ks[7], (N_A, D_CONV, D), D_CONV ** -0.5),
        "kv_mod_w": nrm(ks[8], (D, 2 * D), 0.02),
        "kv_mod_b": nrm(ks[9], (2 * D,), 0.01),
        "kv_norm_g": 1.0 + nrm(ks[10], (D,), 0.02),
        "w_kv": nrm(ks[11], (D, 2 * D_ATTN), D ** -0.5),
        "b_mod_w": nrm(ks[12], (N_B, D, 3 * D), 0.02),
        "b_mod_b": nrm(ks[13], (N_B, 3 * D), 0.01),
        "b_norm_g": 1.0 + nrm(ks[14], (N_B, D), 0.02),
        "b_w_qz": nrm(ks[15], (N_B, D, 2 * D_ATTN), D ** -0.5),
        "b_w_out": nrm(ks[16], (N_B, D_ATTN, D), D_ATTN ** -0.5),
        "final_norm_g": 1.0 + nrm(ks[17], (D,), 0.02),
    }


def reference(x, c, a_mod_w, a_mod_b, a_norm_g, a_w_in, a_conv_w, a_w_out,
              kv_mod_w, kv_mod_b, kv_norm_g, w_kv,
              b_mod_w, b_mod_b, b_norm_g, b_w_qz, b_w_out, final_norm_g):
    k = v = None
    for layer in range(DEPTH):
        if layer < N_A:
            x = short_conv_layer(x, c, a_mod_w[layer], a_mod_b[layer], a_norm_g[layer],
                                 a_w_in[layer], a_conv_w[layer], a_w_out[layer])
        else:
            if layer == N_A:
                kv_shift, kv_scale = modulation(c, kv_mod_w, kv_mod_b, 2)
                hkv = modulated_norm(x, kv_norm_g, kv_shift, kv_scale)
                k_flat, v_flat = jnp.split(hkv @ w_kv, 2, axis=-1)
                k, v = split_heads(k_flat), split_heads(v_flat)
            j = layer - N_A
            x = stick_breaking_layer(x, c, k, v, b_mod_w[j], b_mod_b[j], b_norm_g[j],
                                     b_w_qz[j], b_w_out[j])
    return rmsnorm(x, final_norm_g)
```

```python
import contextlib
import numpy as np
import concourse.bass as bass
import concourse.mybir as mybir
from concourse.bass_utils import run_bass_kernel_spmd

F32 = mybir.dt.float32
BF16 = mybir.dt.bfloat16
AF = mybir.ActivationFunctionType
ALU = mybir.AluOpType

D = 1024
S = 2048
NB = 8
NC_CH = 8
TT = 512
NTT = S // TT
EPS = 1e-6
FUSE_WAITS = True


class _Rec:
    def __init__(self):
        self.call = None

    def __getattr__(self, name):
        def f(*a, **k):
            self.call = (name, a, k)
            return None
        return f


class Sched:
    ENGS = ("pe", "act", "dve", "pool", "sp")

    def __init__(self):
        self.ops = {e: [] for e in self.ENGS}
        self.cnt = {}
        self.waited = {e: {} for e in self.ENGS}
        self.last_w = {}
        self.readers = {}
        self.pending_pe = False

    def _tok_after(self, key, amount):
        self.cnt[key] = self.cnt.get(key, 0) + amount
        return (key, self.cnt[key])

    def _add_wait(self, eng, waits, tok, prod_eng):
        if tok is None:
            return
        key, val = tok
        if prod_eng == "pe" and eng == "pe":
            return
        if self.waited[eng].get(key, 0) >= val:
            return
        waits[key] = max(waits.get(key, 0), val)

    def op(self, eng, fn, reads=(), writes=(), inc=True, dma=None):
        waits = {}
        for r in reads:
            t = self.last_w.get(r)
            if t is not None:
                self._add_wait(eng, waits, t[0], t[1])
        for w in writes:
            t = self.last_w.get(w)
            if t is not None:
                self._add_wait(eng, waits, t[0], t[1])
            for t in self.readers.get(w, ()):
                self._add_wait(eng, waits, t[0], t[1])
        for k, v in waits.items():
            self.waited[eng][k] = v
        if dma is not None:
            key = "dma_" + dma
            tok = self._tok_after(key, 16)
            prod = "dma"
            incspec = (key, 16)
        else:
            key = eng
            if inc:
                tok = self._tok_after(key, 1)
                incspec = (key, 1)
            else:
                assert eng == "pe"
                tok = (key, self.cnt.get(key, 0) + 1)
                incspec = None
            prod = eng
        rec = _Rec()
        fn(rec)
        assert rec.call is not None
        self.ops[eng].append((sorted(waits.items()), rec.call, incspec))
        for w in writes:
            self.last_w[w] = (tok, prod)
            self.readers[w] = []
        for r in reads:
            self.readers.setdefault(r, []).append((tok, prod))
        return tok

    def wait_all(self, eng, keys=None):
        waits = {}
        for k, v in self.cnt.items():
            if keys is not None and k not in keys:
                continue
            if self.waited[eng].get(k, 0) < v:
                waits[k] = v
                self.waited[eng][k] = v
        self.ops[eng].append((sorted(waits.items()), None, None))

    def emit(self, nc, stack):
        sems = {}
        for k in self.cnt:
            sems[k] = stack.enter_context(nc.semaphore("s_" + k))
        block = stack.enter_context(nc.Block())

        FUSABLE = ("activation", "tensor_tensor", "tensor_scalar", "scalar_tensor_tensor", "tensor_copy", "reciprocal")

        def run(e, name):
            for waits, fn, incspec in self.ops[name]:
                fused = None
                if (FUSE_WAITS and fn is not None and waits and name in ("act", "dve", "pool") and fn[0] in FUSABLE
                        and fn[2].get("accum_out") is None):
                    fused = waits[-1]
                    waits = waits[:-1]
                for k, v in waits:
                    e.wait_ge(sems[k], v)
                if fn is None:
                    continue
                name_, a_, k_ = fn
                ins = getattr(e, name_)(*a_, **k_)
                if fused is not None:
                    ins._wait_ge(sems[fused[0]], fused[1])
                if incspec is not None:
                    ins.then_inc(sems[incspec[0]], incspec[1])

        @block.tensor
        def _(e):
            run(e, "pe")

        @block.scalar
        def _(e):
            run(e, "act")

        @block.vector
        def _(e):
            run(e, "dve")

        @block.gpsimd
        def _(e):
            run(e, "pool")

        @block.sync
        def _(e):
            run(e, "sp")


class _Stop(Exception):
    pass


class Carve:
    def __init__(self, t):
        self.t = t
        self.n = t.shape[1]
        self.off = 0

    def reset(self):
        self.off = 0

    def alloc(self, shape, dt):
        n = int(np.prod(shape))
        n16 = n * (2 if dt == F32 else 1)
        n16 = (n16 + 15) // 16 * 16
        assert self.off + n16 <= self.n, ("scratch overflow", self.off, n16, self.n)
        ap = self.t[:, self.off:self.off + n16]
        self.off += n16
        if dt == F32:
            ap = ap.bitcast(F32)
        ap = ap[:, 0:n]
        if len(shape) == 2:
            ap = ap.rearrange("p (a b) -> p a b", a=shape[0])
        elif len(shape) == 3:
            ap = ap.rearrange("p (a b c) -> p a b c", a=shape[0], b=shape[1])
        return ap


def build_nc(debug=False):
    nc = bass.Bass("TRN2", target_bir_lowering=False)
    dram = lambda name, shape, kind: nc.dram_tensor(name, shape, F32, kind=kind).ap()
    xT_d = dram("xT", [128, 8, S], "ExternalInput")
    crep_d = dram("crep", [128, D], "ExternalInput")
    modT_d = dram("modT", [24, 128, D], "ExternalInput")
    modW2_d = dram("modW2", [10, 128, 8, 512], "ExternalInput")
    cT_d = dram("cT", [128, 8], "ExternalInput")
    vecs_d = dram("vecs", [128, 120], "ExternalInput")
    wA_d = dram("wA", [8, 128, 8, 512], "ExternalInput")
    wAo_d = dram("wAo", [8, 128, D], "ExternalInput")
    wB_d = dram("wB", [8, 128, 8, 512], "ExternalInput")
    wBo_d = dram("wBo", [8, 128, D], "ExternalInput")
    out_d = dram("outT", [128, 8, S], "ExternalOutput")

    Q = Sched()
    try:
      with contextlib.ExitStack() as st:
        sb = lambda name, shape, dt: st.enter_context(nc.sbuf_tensor(name, shape, dt))
        xT = sb("xT_sb", [128, 8, S], F32)
        xn = sb("xn_sb", [128, 8, S], BF16)
        crep = sb("crep_sb", [128, D], F32)
        vecs = sb("vecs_sb", [128, 120], F32)
        mod = sb("mod_sb", [128, 64], F32)
        der = sb("der_sb", [128, 64], F32)
        derb = sb("derb_sb", [128, 32], BF16)
        shbc = sb("shbc_sb", [128, 8, 128], BF16)
        cTs = sb("cT_sb", [128, 8], F32)
        scb = sb("scb_sb", [128, 8], BF16)
        NSTG = 4
        wch = sb("wch_sb", [128, 2, 8, 512], BF16)
        wo = sb("wo_sb", [128, 2, D], BF16)
        cst = sb("cst_sb", [128, 3, 128], BF16)
        sq = sb("sq_sb", [128, 3, TT], BF16)
        rstd = sb("rstd_sb", [128, 2, TT], F32)
        lnt = sb("lnt_sb", [128, TT], F32)
        scratch = sb("scratch_sb", [128, 38000], BF16)
        ps = st.enter_context(nc.psum_tensor("ps", [128, 8, 512], F32))
        cv = Carve(scratch)
        stage_raw = cv.alloc([1, NSTG * D * 2], BF16)[:, 0, :]
        stage = stage_raw.bitcast(F32).rearrange("p (a b) -> p a b", a=NSTG)
        modw = stage_raw.rearrange("p (a k n) -> p a k n", a=2, k=8)
        junk = cv.alloc([1, D], F32)[:, 0, :]
        cu = cv.alloc([2, S + 2], F32)
        csb = cv.alloc([2, TT], F32)
        szb = cv.alloc([2, TT], F32)
        acc = cv.alloc([2, TT], F32)
        yj = cv.alloc([4, TT], BF16)
        bias4 = cv.alloc([4, 4], F32)
        wchX = cv.alloc([2, 8, 512], BF16)
        woX = cv.alloc([2, D], BF16)

        def WCH(k):
            return wch[:, k] if k < 2 else wchX[:, k - 2]

        def WO(k):
            return wo[:, k] if k < 2 else woX[:, k - 2]

        U = cst[:, 0, :]
        SL = cst[:, 1, :]
        ONES = cst[:, 2, :]

        def dbg_dump(tag, src_ap, dst_ap, reads):
            if debug != tag:
                return
            for eng in Sched.ENGS:
                Q.wait_all(eng)
            if not isinstance(src_ap, list):
                src_ap, dst_ap = [src_ap], [dst_ap]
            for sa, da in zip(src_ap, dst_ap):
                Q.op("sp", lambda e, sa=sa, da=da: e.dma_start(out=da, in_=sa), reads=reads, dma="out")
            Q.wait_all("sp", keys=["dma_out"])
            Q.emit(nc, st)
            raise _Stop()

        GMA, GMKV, GMB, NBZ = 0, 8, 16, 24

        Q.op("pool", lambda e: e.memset(cst[:], 1.0), writes=["cst"])
        Q.op("pool", lambda e: e.affine_select(out=U, in_=U, pattern=[[-1, 128]], compare_op=ALU.is_ge,
                                               fill=0.0, base=0, channel_multiplier=1), reads=["cst"], writes=["cst"])
        Q.op("pool", lambda e: e.affine_select(out=SL, in_=SL, pattern=[[1, 128]], compare_op=ALU.is_gt,
                                               fill=0.0, base=0, channel_multiplier=-1), reads=["cst"], writes=["cst"])

        Q.op("sp", lambda e: e.dma_start(out=crep[:], in_=crep_d), writes=["crep"], dma="c")
        Q.op("sp", lambda e: e.dma_start(out=vecs[:], in_=vecs_d), writes=["vecs"], dma="v")

        mod_next = [0]

        def mod_dma():
            j = mod_next[0]
            if j >= 24:
                return
            if j == 16:
                for t_ in range(1, NTT):
                    x_dma(t_)
            mod_next[0] += 1
            Q.op("sp", lambda e, j=j: e.dma_start(out=stage[:, j % NSTG, :], in_=modT_d[j]),
                 writes=["stage%d" % (j % NSTG)], dma="m%d" % (j % NSTG))

        mod_done = [0]

        def mod_reduce():
            j = mod_done[0]
            if j >= 24:
                return
            mod_done[0] += 1
            if j < 64:
                Q.op("dve", lambda e, j=j: e.scalar_tensor_tensor(
                    out=junk[:], in0=stage[:, j % NSTG, :], scalar=1.0, in1=crep[:], op0=ALU.mult, op1=ALU.mult,
                    accum_out=mod[:, j:j + 1]), reads=["stage%d" % (j % NSTG), "crep"], writes=["junk", "mod%d" % (j // 8)])
            else:
                Q.op("pool", lambda e: e.tensor_tensor(out=junk2[:], in0=stage[:, j % NSTG, :], in1=crep[:], op=ALU.mult),
                     reads=["stage%d" % (j % NSTG), "crep"], writes=["junk2"])
                Q.op("pool", lambda e: e.tensor_reduce(out=mod[:, j:j + 1], in_=junk2[:], axis=mybir.AxisListType.X, op=ALU.add),
                     reads=["junk2"], writes=["mod%d" % (j // 8)])
            mod_dma()

        def x_dma(t):
            tl_ = slice(t * TT, (t + 1) * TT)
            Q.op("sp", lambda e: e.dma_start(out=xT[:, :, tl_], in_=xT_d[:, :, tl_]),
                 writes=[("x", c, t) for c in range(8)], dma="x%d" % t)

        x_dma(0)
        for _ in range(NSTG):
            mod_dma()

        def wload(src_d, srco_d, j, buf):
            for h in range(2):
                Q.op("pool", lambda e, h=h: e.dma_start(out=WCH(buf)[:, 4 * h:4 * h + 4, :], in_=src_d[j, :, 4 * h:4 * h + 4, :]),
                     writes=["wch%d" % buf], dma="wc%d" % buf)
            Q.op("pool", lambda e: e.dma_start(out=WO(buf), in_=srco_d[j]), writes=["wo%d" % buf], dma="wo%d" % buf)

        for j_ in range(2):
            wload(wA_d, wAo_d, j_, j_)

        Q.op("sp", lambda e: e.dma_start(out=cTs[:], in_=cT_d), writes=["cTs"], dma="ct")
        Q.op("act", lambda e: e.activation(out=scb[:], in_=cTs[:], func=AF.Silu), reads=["cTs"], writes=["scb"])

        m2_dma_i = [0]
        m2_pe_i = [0]

        def m2_dma():
            i = m2_dma_i[0]
            if i >= 10:
                return
            m2_dma_i[0] += 1
            a = i % 2
            for h in range(2):
                Q.op("pool", lambda e: e.dma_start(out=modw[:, a, 4 * h:4 * h + 4, :], in_=modW2_d[i, :, 4 * h:4 * h + 4, :]),
                     writes=["stage%d" % (2 * a), "stage%d" % (2 * a + 1)], dma="mw%d" % a)

        def m2_compute():
            i = m2_pe_i[0]
            if i >= 10:
                return
            m2_pe_i[0] += 1
            a = i % 2
            for g in range(4):
                for kc in range(8):
                    Q.op("pe", lambda e: e.matmul(ps[:, 7, 8 + g:9 + g], lhsT=modw[:, a, kc, g * 128:(g + 1) * 128], rhs=scb[:, kc:kc + 1],
                                                  start=(kc == 0), stop=(kc == 7)),
                         reads=["stage%d" % (2 * a), "stage%d" % (2 * a + 1), "scb"], writes=["ps7"], inc=(kc == 7 and g == 3))
            j0 = 24 + 4 * i
            Q.op("dve", lambda e: e.tensor_tensor(out=mod[:, j0:j0 + 4], in0=ps[:, 7, 8:12], in1=vecs[:, j0:j0 + 4], op=ALU.add),
                 reads=["ps7", "vecs"], writes=["mod%d" % (j0 // 8)])
            m2_dma()

        Q.op("act", lambda e: e.activation(out=crep[:], in_=crep[:], func=AF.Silu), reads=["crep"], writes=["crep"])

        def mod_finish(g0, g1):
            Q.op("dve", lambda e: e.tensor_tensor(out=mod[:, 8 * g0:8 * g1], in0=mod[:, 8 * g0:8 * g1],
                                                  in1=vecs[:, 8 * g0:8 * g1], op=ALU.add),
                 reads=["mod%d" % g for g in range(g0, g1)] + ["vecs"], writes=["mod%d" % g for g in range(g0, g1)])

        def gmod(dst, scale_col, gwhich, grp):
            Q.op("dve", lambda e: e.scalar_tensor_tensor(out=der[:, dst:dst + 8], in0=mod[:, scale_col:scale_col + 8], scalar=1.0,
                                                         in1=vecs[:, 64 + 8 * gwhich:72 + 8 * gwhich], op0=ALU.add, op1=ALU.mult),
                 reads=["mod%d" % grp, "vecs"], writes=["der%d" % (dst // 8)])

        def shiftbf(dst, shift_col, grp):
            Q.op("dve", lambda e: e.tensor_copy(out=derb[:, dst:dst + 8], in_=mod[:, shift_col:shift_col + 8]),
                 reads=["mod%d" % grp], writes=["derb%d" % (dst // 8)])

        for _ in range(16):
            mod_reduce()
        mod_finish(0, 2)
        gmod(GMA, 8, 0, 1)
        shiftbf(0, 0, 0)
        dbg_dump("mod", mod[:], out_d[:, 0, 0:64], ["mod0", "mod1", "mod2"])

        stat_i = [0]

        def stats(tt, bank):
            for c in range(8):
                k = stat_i[0] % 3
                stat_i[0] += 1
                Q.op("act", lambda e, c=c, k=k: e.activation(out=sq[:, k, :], in_=xT[:, c, tt * TT:(tt + 1) * TT], func=AF.Square),
                     reads=[("x", c, tt)], writes=["sq%d" % k])
                Q.op("pe", lambda e, c=c, k=k: e.matmul(ps[:, bank, :], lhsT=ONES, rhs=sq[:, k, :], start=(c == 0), stop=(c == 7)),
                     reads=["sq%d" % k, "cst"], writes=["ps%d" % bank], inc=True)
            r = tt % 2
            Q.op("act", lambda e: e.activation(out=lnt[:], in_=ps[:, bank, :], func=AF.Ln, scale=1.0 / D, bias=EPS),
                 reads=["ps%d" % bank], writes=["lnt"])
            Q.op("act", lambda e: e.activation(out=rstd[:, r, :], in_=lnt[:], func=AF.Exp, scale=-0.5),
                 reads=["lnt"], writes=["rstd%d" % r])
            return r

        def normalize(tt, bank):
            r = stats(tt, bank)
            tl_ = slice(tt * TT, (tt + 1) * TT)
            for c4 in range(2):
                cs_ = slice(4 * c4, 4 * c4 + 4)
                Q.op("dve", lambda e: e.tensor_tensor(out=xn[:, cs_, tl_], in0=xT[:, cs_, tl_],
                                                      in1=rstd[:, r, None, :].to_broadcast([128, 4, TT]), op=ALU.mult),
                     reads=[("x", c, tt) for c in range(4 * c4, 4 * c4 + 4)] + ["rstd%d" % r], writes=[("xn", tt)])

        normalize(0, 7)
        if debug == "xn":
            for tt in range(1, NTT):
                normalize(tt, 7)
        if debug == "xn":
            Q.op("dve", lambda e: e.tensor_copy(out=xT[:], in_=xn[:]), reads=[("xn", t) for t in range(4)],
                 writes=[("x", c, t) for c in range(8) for t in range(4)])
        dbg_dump("xn", xT[:], out_d, [("x", c, t) for c in range(8) for t in range(4)])

        for b in range(2):
            Q.op("dve", lambda e, b=b: e.memset(cu[:, b, 0:2], 0.0), writes=["cu%d" % b])

        def chunk_prep(buf, groups, pbank):
            ng = (groups[-1][1]) // 128
            for g in range(ng):
                sc = [s for (c0, c1, s, gm) in groups if c0 <= g * 128 < c1][0]
                for kc in range(8):
                    Q.op("pe", lambda e, g=g, kc=kc, sc=sc: e.matmul(ps[:, pbank, g:g + 1], lhsT=WCH(buf)[:, kc, g * 128:(g + 1) * 128],
                                                                    rhs=derb[:, sc + kc:sc + kc + 1], start=(kc == 0), stop=(kc == 7)),
                         reads=["wch%d" % buf, "derb%d" % (sc // 8)], writes=["ps%d" % pbank], inc=(kc == 7 and g == ng - 1))

        def chunk_fold(buf, groups):
            for (c0, c1, s, gm) in groups:
                for kc in range(8):
                    Q.op("dve", lambda e, kc=kc, c0=c0, c1=c1, gm=gm: e.tensor_scalar(
                        out=WCH(buf)[:, kc, c0:c1], in0=WCH(buf)[:, kc, c0:c1], scalar1=der[:, gm + kc:gm + kc + 1], scalar2=None,
                        op0=ALU.mult), reads=["wch%d" % buf, "der%d" % (gm // 8)], writes=["wch%d" % buf])

        it = 0
        GORDER = [3, 1, 2, 0]
        pend = [None]

        def outproj_A(pjp, ptt, pr0, pr1, jos=range(8), tail=True):
            ptl = slice(ptt * TT, (ptt + 1) * TT)
            b0, b1 = (2 * pjp) % 4, (2 * pjp + 1) % 4
            for jo in jos:
                ob = 6 + (jo % 2)
                Q.op("pe", lambda e: e.matmul(ps[:, ob, :], lhsT=WO(b0)[:, jo * 128:(jo + 1) * 128], rhs=yj[:, pr0, :],
                                              start=True, stop=False),
                     reads=["wo%d" % b0, "yj%d" % pr0], writes=["ps%d" % ob], inc=False)
                Q.op("pe", lambda e: e.matmul(ps[:, ob, :], lhsT=WO(b1)[:, jo * 128:(jo + 1) * 128], rhs=yj[:, pr1, :],
                                              start=False, stop=True),
                     reads=["wo%d" % b1, "yj%d" % pr1], writes=["ps%d" % ob])
                Q.op("dve", lambda e: e.scalar_tensor_tensor(out=xT[:, jo, ptl], in0=ps[:, ob, :], scalar=mod[:, 16 + jo:17 + jo],
                                                             in1=xT[:, jo, ptl], op0=ALU.mult, op1=ALU.add),
                     reads=["ps%d" % ob, "mod2", ("x", jo, ptt)], writes=[("x", jo, ptt)])
            if tail and ptt == NTT - 1:
                for jj_ in range(2):
                    pj = 2 * pjp + jj_
                    if pj + 4 < 8:
                        wload(wA_d, wAo_d, pj + 4, pj % 4)
                    elif pjp == 2:
                        wload(wB_d, wBo_d, jj_, jj_)

        grpA = [(0, 512, 0, GMA)]

        def prep_A(jp_):
            for jj_ in range(2):
                buf_ = (2 * jp_ + jj_) % 4
                chunk_prep(buf_, grpA, 7)
                Q.op("dve", lambda e: e.tensor_copy(out=bias4[:, buf_, :], in_=ps[:, 7, 0:4]), reads=["ps7"], writes=["bias4%d" % buf_])
                chunk_fold(buf_, grpA)

        prep_A(0)
        for jp in range(4):
            for tt in range(NTT):
                tl = slice(tt * TT, (tt + 1) * TT)
                rr = []
                for jj in range(2):
                    j = 2 * jp + jj
                    buf = j % 4
                    cb = j % 2
                    if jp == 0 and jj == 1 and tt + 1 < NTT and debug != "xn":
                        normalize(tt + 1, 7)
                    banks = [(4 * it + g) % 6 for g in range(4)]
                    for gi, g in enumerate(GORDER):
                        for kc in range(8):
                            Q.op("pe", lambda e: e.matmul(ps[:, banks[g], :], lhsT=WCH(buf)[:, kc, g * 128:(g + 1) * 128],
                                                          rhs=xn[:, kc, tl], start=(kc == 0), stop=(kc == 7)),
                                 reads=["wch%d" % buf, ("xn", tt)], writes=["ps%d" % banks[g]], inc=(kc == 7))
                        if jj == 0 and pend[0] is not None:
                            outproj_A(*pend[0], jos=(2 * gi, 2 * gi + 1), tail=(gi == 3))
                    r = it % 2
                    r4 = it % 4
                    rr.append(r4)
                    bb, bc, bu, bz = [bias4[:, buf, g:g + 1] for g in range(4)]
                    Q.op("act", lambda e: e.activation(out=szb[:, r, :], in_=ps[:, banks[3], :], func=AF.Silu, bias=bz, scale=1.0),
                         reads=["ps%d" % banks[3], "bias4%d" % buf], writes=["sz%d" % r])
                    Q.op("act", lambda e: e.activation(out=csb[:, r, :], in_=ps[:, banks[1], :], func=AF.Identity, bias=bc, scale=1.0),
                         reads=["ps%d" % banks[1], "bias4%d" % buf], writes=["csb%d" % r])
                    if jj == 0:
                        pend[0] = None
                    cut = cu[:, cb, 2 + tt * TT:2 + (tt + 1) * TT]
                    Q.op("dve", lambda e: e.scalar_tensor_tensor(out=cut, in0=ps[:, banks[2], :], scalar=bu, in1=csb[:, r, :],
                                                                 op0=ALU.add, op1=ALU.mult),
                         reads=["ps%d" % banks[2], "csb%d" % r, "bias4%d" % buf], writes=["cu%d" % cb])
                    w0, w1, w2 = [vecs[:, 96 + 8 * k + j:97 + 8 * k + j] for k in range(3)]
                    Q.op("act", lambda e: e.activation(out=acc[:, r, :], in_=cut, func=AF.Copy, scale=w2),
                         reads=["cu%d" % cb, "vecs"], writes=["acc%d" % r])
                    Q.op("dve", lambda e: e.scalar_tensor_tensor(out=acc[:, r, :], in0=cu[:, cb, 1 + tt * TT:1 + (tt + 1) * TT], scalar=w1,
                                                                 in1=acc[:, r, :], op0=ALU.mult, op1=ALU.add),
                         reads=["cu%d" % cb, "acc%d" % r, "vecs"], writes=["acc%d" % r])
                    Q.op("dve", lambda e: e.scalar_tensor_tensor(out=acc[:, r, :], in0=cu[:, cb, tt * TT:(tt + 1) * TT], scalar=w0,
                                                                 in1=acc[:, r, :], op0=ALU.mult, op1=ALU.add),
                         reads=["cu%d" % cb, "acc%d" % r, "vecs"], writes=["acc%d" % r])
                    Q.op("dve", lambda e: e.scalar_tensor_tensor(out=acc[:, r, :], in0=ps[:, banks[0], :], scalar=bb, in1=acc[:, r, :],
                                                                 op0=ALU.add, op1=ALU.mult),
                         reads=["ps%d" % banks[0], "acc%d" % r, "bias4%d" % buf], writes=["acc%d" % r])
                    Q.op("pool", lambda e: e.tensor_tensor(out=yj[:, r4, :], in0=acc[:, r, :], in1=szb[:, r, :], op=ALU.mult),
                         reads=["acc%d" % r, "sz%d" % r], writes=["yj%d" % r4])
                    if it < 2:
                        for _ in range(4):
                            mod_reduce()
                        if it == 1:
                            mod_finish(2, 3)
                            m2_dma()
                            m2_dma()
                        wload(wA_d, wAo_d, 2 + it, 2 + it)
                    elif it % 2 == 1:
                        m2_compute()
                    if tt == NTT - 1 and jj == 1 and jp + 1 < 4:
                        prep_A(jp + 1)
                    it += 1
                pend[0] = (jp, tt, rr[0], rr[1])
        outproj_A(*pend[0])
        dbg_dump("A", xT[:], out_d, [("x", c, t) for c in range(8) for t in range(4)])
        while m2_pe_i[0] < 10:
            m2_compute()
        gmod(GMKV, 32, 1, 4)
        gmod(GMB, 48, 2, 6)
        shiftbf(8, 24, 3)
        shiftbf(16, 40, 5)
        for kc in range(8):
            Q.op("dve", lambda e, kc=kc: e.tensor_copy(out=shbc[:, kc, :], in_=mod[:, 24 + kc:25 + kc].to_broadcast([128, 128])),
                 reads=["mod3"], writes=["shbc"])

        for tt in range(NTT):
            normalize(tt, 7)

        for eng in Sched.ENGS:
            Q.wait_all(eng)

        cv.reset()
        kT = cv.alloc([2, S], BF16)
        qT = cv.alloc([2, S], BF16)
        zg = cv.alloc([2, S], BF16)
        Vt = cv.alloc([2, 16, 128], BF16)
        ozb = cv.alloc([2, TT], BF16)
        Eb = cv.alloc([4, 2, TT], BF16)
        Lb = cv.alloc([3, 2, TT], BF16)
        Pb = cv.alloc([2, 2, TT], BF16)
        Ab = cv.alloc([2, 2, TT], BF16)
        ez = cv.alloc([4, TT], F32)
        vbias = cv.alloc([2, 128], F32)
        bias3 = cv.alloc([2, 4], F32)
        nbz = cv.alloc([2, 1], F32)
        zbuf = cv.alloc([4, TT], F32)
        neg1 = None

        prj_i = [0]

        def prj_bank():
            prj_i[0] += 1
            return 6 + (prj_i[0] % 2)

        class Item:
            def __init__(self, stages, cost=1.7, barrier=False, is_out=False, nbanks=1, release_stage=1, after=()):
                self.stages = stages
                self.cost = cost
                self.barrier = barrier
                self.is_out = is_out
                self.nbanks = nbanks
                self.banks = []
                self.release_stage = release_stage
                self.after = list(after)
                self.done = False

        def pair_prep_item(jh):
            buf = jh % 2
            grpB = [(0, 256, 8, GMKV), (256, 512, 16, GMB)]
            st = {}

            def s0():
                st["pb"], st["pv"] = st["item"].banks
                chunk_prep(buf, grpB, st["pb"])
                pv = st["pv"]
                for kc in range(8):
                    Q.op("pe", lambda e: e.matmul(ps[:, pv, 0:128], lhsT=shbc[:, kc, :], rhs=wch[:, buf, kc, 128:256],
                                                  start=(kc == 0), stop=(kc == 7)),
                         reads=["shbc", "wch%d" % buf], writes=["ps%d" % pv], inc=(kc == 7))

            def s1():
                pb, pv = st["pb"], st["pv"]
                Q.op("dve", lambda e: e.tensor_copy(out=bias3[:, buf, :], in_=ps[:, pb, 0:4]), reads=["ps%d" % pb], writes=["bias3%d" % buf])
                Q.op("dve", lambda e: e.tensor_scalar(out=nbz[:, buf, :], in0=bias3[:, buf, 3:4], scalar1=-1.0, scalar2=None, op0=ALU.mult),
                     reads=["bias3%d" % buf], writes=["nbz%d" % buf])
                Q.op("dve", lambda e: e.tensor_copy(out=vbias[:, buf, :], in_=ps[:, pv, 0:128]), reads=["ps%d" % pv], writes=["vbias%d" % buf])

            def fold_stage(k0):
                def f():
                    for (c0, c1, s_, gm) in grpB:
                        for kc in range(k0, k0 + 2):
                            Q.op("dve", lambda e: e.tensor_scalar(out=wch[:, buf, kc, c0:c1], in0=wch[:, buf, kc, c0:c1],
                                                                  scalar1=der[:, gm + kc:gm + kc + 1], scalar2=None, op0=ALU.mult),
                                 reads=["wch%d" % buf, "der%d" % (gm // 8)], writes=["wch%d" % buf])
                return f

            st["item"] = Item([s0, s1] + [fold_stage(k) for k in (0, 2, 4, 6)] + [lambda: None], cost=0.8, barrier=True, nbanks=2)
            return st["item"]

        ztile_i = [0]

        def proj_items(jh):
            buf = jh % 2
            items = []

            def feat_tile(g, tt, kind):
                st = {}
                tl = slice(tt * TT, (tt + 1) * TT)
                bcol = bias3[:, buf, g:g + 1]

                def s0():
                    st["pb"] = st["item"].banks[0]
                    pb = st["pb"]
                    for kc in range(8):
                        Q.op("pe", lambda e: e.matmul(ps[:, pb, :], lhsT=wch[:, buf, kc, g * 128:(g + 1) * 128], rhs=xn[:, kc, tl],
                                                      start=(kc == 0), stop=(kc == 7)),
                             reads=["wch%d" % buf, ("xn", tt)], writes=["ps%d" % pb], inc=(kc == 7))

                def s1():
                    pb = st["pb"]
                    if kind == "k":
                        Q.op("dve", lambda e: e.tensor_scalar(out=kT[:, buf, tl], in0=ps[:, pb, :], scalar1=bcol, scalar2=None, op0=ALU.add),
                             reads=["ps%d" % pb, "bias3%d" % buf], writes=["kT%d" % buf])
                    elif kind == "q":
                        Q.op("dve", lambda e: e.tensor_scalar(out=qT[:, buf, tl], in0=ps[:, pb, :], scalar1=bcol, scalar2=None, op0=ALU.add),
                             reads=["ps%d" % pb, "bias3%d" % buf], writes=["qT%d" % buf])
                    else:
                        st["r"] = ztile_i[0] % 4
                        ztile_i[0] += 1
                        r = st["r"]
                        Q.op("act", lambda e: e.activation(out=ez[:, r, :], in_=ps[:, pb, :], func=AF.Exp, scale=-1.0, bias=nbz[:, buf, :]),
                             reads=["ps%d" % pb, "nbz%d" % buf], writes=["ez%d" % r])
                        Q.op("dve", lambda e: e.tensor_scalar(out=zbuf[:, r, :], in0=ps[:, pb, :], scalar1=bcol, scalar2=None, op0=ALU.add),
                             reads=["ps%d" % pb, "bias3%d" % buf, "ez%d" % r], writes=["zbuf%d" % r])

                def s2():
                    r = st["r"]
                    if ZPOOL:
                        Q.op("pool", lambda e: e.tensor_scalar(out=ez[:, r, :], in0=ez[:, r, :], scalar1=1.0, scalar2=None, op0=ALU.add),
                             reads=["ez%d" % r], writes=["ez%d" % r])
                        Q.op("pool", lambda e: e.tensor_tensor(out=ez[:, r, :], in0=ez[:, r, :], in1=neg1, op=ALU.pow),
                             reads=["ez%d" % r, "neg1"], writes=["ez%d" % r])
                        Q.op("pool", lambda e: e.tensor_tensor(out=zg[:, buf, tl], in0=zbuf[:, r, :], in1=ez[:, r, :], op=ALU.mult),
                             reads=["ez%d" % r, "zbuf%d" % r], writes=["zg%d" % buf])
                    else:
                        Q.op("dve", lambda e: e.tensor_scalar(out=ez[:, r, :], in0=ez[:, r, :], scalar1=1.0, scalar2=None, op0=ALU.add),
                             reads=["ez%d" % r], writes=["ez%d" % r])

                def rec_stage(q4):
                    def f():
                        r = st["r"]
                        cs = slice(q4 * 128, (q4 + 1) * 128)
                        Q.op("dve", lambda e: e.reciprocal(out=ez[:, r, cs], in_=ez[:, r, cs]), reads=["ez%d" % r], writes=["ez%d" % r])
                    return f

                def s7():
                    r = st["r"]
                    Q.op("dve", lambda e: e.tensor_tensor(out=zg[:, buf, tl], in0=zbuf[:, r, :], in1=ez[:, r, :], op=ALU.mult),
                         reads=["ez%d" % r, "zbuf%d" % r], writes=["zg%d" % buf])

                zst = [s2] + ([] if ZPOOL else [rec_stage(q4) for q4 in range(4)] + [s7])
                st["item"] = Item([s0, s1] + (zst if kind == "z" else []), cost=1.7)
                return st["item"]

            def v_tile(i4):
                st = {}

                def s0():
                    st["pb"] = st["item"].banks[0]
                    pb = st["pb"]
                    for ii in range(4):
                        i = i4 * 4 + ii
                        for kc in range(8):
                            Q.op("pe", lambda e: e.matmul(ps[:, pb, ii * 128:(ii + 1) * 128], lhsT=xn[:, kc, i * 128:(i + 1) * 128],
                                                          rhs=wch[:, buf, kc, 128:256], start=(kc == 0), stop=(kc == 7)),
                                 reads=["wch%d" % buf, ("xn", i // 4)], writes=["ps%d" % pb], inc=(kc == 7 and ii == 3))

                def s1():
                    pb = st["pb"]
                    Q.op("dve", lambda e: e.tensor_tensor(out=Vt[:, buf, i4 * 4:(i4 + 1) * 4, :],
                                                          in0=ps[:, pb, :].rearrange("p (a b) -> p a b", a=4),
                                                          in1=vbias[:, buf, None, :].to_broadcast([128, 4, 128]), op=ALU.add),
                         reads=["ps%d" % pb, "vbias%d" % buf], writes=["Vt%d" % buf])
                st["item"] = Item([s0, s1], cost=1.9)
                return st["item"]

            for tt in range(NTT):
                items.append(feat_tile(0, tt, "k"))
                items.append(v_tile(tt))
                items.append(feat_tile(2, tt, "q"))
                items.append(feat_tile(3, tt, "z"))
            return items

        def outproj_items(jh, qt, r):
            buf = jh % 2
            tl = slice(qt * TT, (qt + 1) * TT)
            items = []
            for jo in range(8):
                def mk(jo=jo):
                    st = {}

                    def s0():
                        pb = st["item"].banks[0]
                        Q.op("pe", lambda e: e.matmul(ps[:, pb, :], lhsT=wo[:, buf, jo * 128:(jo + 1) * 128], rhs=ozb[:, r, :],
                                                      start=True, stop=True),
                             reads=["wo%d" % buf, "oz%d" % r], writes=["ps%d" % pb])

                    def s1():
                        pb = st["item"].banks[0]
                        Q.op("dve", lambda e: e.scalar_tensor_tensor(out=xT[:, jo, tl], in0=ps[:, pb, :], scalar=mod[:, 56 + jo:57 + jo],
                                                                     in1=xT[:, jo, tl], op0=ALU.mult, op1=ALU.add),
                             reads=["ps%d" % pb, "mod7", ("x", jo, qt)], writes=[("x", jo, qt)])
                    st["item"] = Item([s0, s1], cost=0.25, is_out=True)
                    return st["item"]
                items.append(mk())
            return items

        def final_item(tt, deps):
            tl = slice(tt * TT, (tt + 1) * TT)
            st = {}
            r = tt % 2

            def sq_ops(cs):
                for c in cs:
                    k = stat_i[0] % 3
                    stat_i[0] += 1
                    st[c] = k
                    Q.op("dve", lambda e: e.tensor_tensor(out=sq[:, k, :], in0=xT[:, c, tl], in1=xT[:, c, tl], op=ALU.mult),
                         reads=[("x", c, tt)], writes=["sq%d" % k])

            def mm_ops(cs):
                bank = st["bank"]
                for c in cs:
                    k = st[c]
                    Q.op("pe", lambda e: e.matmul(ps[:, bank, :], lhsT=ONES, rhs=sq[:, k, :], start=(c == 0), stop=(c == 7)),
                         reads=["sq%d" % k, "cst"], writes=["ps%d" % bank], inc=True)

            def s0():
                st["bank"] = st["item"].banks[0]
                sq_ops((0, 1, 2))

            def s1():
                mm_ops((0, 1, 2))
                sq_ops((3, 4, 5))

            def s2():
                mm_ops((3, 4, 5))
                sq_ops((6, 7))

            def s3():
                mm_ops((6, 7))

            def s4():
                bank = st["bank"]
                Q.op("act", lambda e: e.activation(out=lnt[:], in_=ps[:, bank, :], func=AF.Ln, scale=1.0 / D, bias=EPS),
                     reads=["ps%d" % bank], writes=["lnt"])
                Q.op("act", lambda e: e.activation(out=rstd[:, r, :], in_=lnt[:], func=AF.Exp, scale=-0.5),
                     reads=["lnt"], writes=["rstd%d" % r])

            def stt(cs):
                def f():
                    for c in cs:
                        Q.op("dve", lambda e: e.scalar_tensor_tensor(out=xT[:, c, tl], in0=xT[:, c, tl], scalar=vecs[:, 88 + c:89 + c],
                                                                     in1=rstd[:, r, :], op0=ALU.mult, op1=ALU.mult),
                             reads=[("x", c, tt), "vecs", "rstd%d" % r], writes=[("x", c, tt)])
                return f

            def s7():
                stt((6, 7))()
                Q.op("sp", lambda e: e.dma_start(out=out_d[:, :, tl], in_=xT[:, :, tl]),
                     reads=[("x", c, tt) for c in range(8)], dma="out")

            st["item"] = Item([s0, s1, s2, s3, s4, stt((0, 1, 2)), stt((3, 4, 5)), s7], cost=0.5, release_stage=4, after=deps)
            return st["item"]

        NE, NL = 4, 3
        units = []
        for jh in range(8):
            for qt in range(4):
                for kb in range(4 * qt + 3, -1, -1):
                    units.append((jh, qt, kb))
        n = len(units)
        if debug == "B1":
            n = 40

        def c0_of(u):
            jh, qt, kb = units[u]
            o = kb - 4 * qt
            return 128 * o if o > 0 else 0

        def first(u):
            return units[u][2] == 4 * units[u][1] + 3

        def last(u):
            return units[u][2] == 0

        bg = []
        chain_i = [0]
        chain_of = {}

        inflight = []

        free_banks = [6, 7, 5]

        def bg_advance():
            for ent in list(inflight):
                ent[0].stages[ent[1]]()
                if ent[1] == ent[0].release_stage:
                    free_banks.extend(ent[0].banks)
                    ent[0].banks = []
                ent[1] += 1
                if ent[1] >= len(ent[0].stages):
                    ent[0].done = True
                    inflight.remove(ent)

        def bg_start(step, budget, force=False):
            while bg and (force or bg[0][0] <= step) and budget > 0:
                if any(ent[0].barrier for ent in inflight):
                    break
                item = bg[0][1]
                nb_ = item.nbanks if len(item.stages) > 1 else 0
                if len(free_banks) < nb_:
                    break
                if any(not d_.done for d_ in item.after):
                    break
                bg.pop(0)
                item.banks = [free_banks.pop(0) for _ in range(nb_)]
                budget -= item.cost
                item.stages[0]()
                if len(item.stages) > 1:
                    inflight.append([item, 1])
                else:
                    item.done = True
            return budget

        def bg_flush(step):
            while True:
                bg_advance()
                bg_start(step, 100.0, force=False)
                if not inflight and not (bg and bg[0][0] <= step):
                    break

        def bg_flush_all():
            while bg or inflight:
                bg_advance()
                bg_start(0, 100.0, force=True)

        def st_QK(u):
            jh, qt, kb = units[u]
            buf = jh % 2
            c0 = c0_of(u)
            ql = slice(qt * TT + c0, (qt + 1) * TT)
            for hh in range(2):
                hs = slice(hh * 64, (hh + 1) * 64)
                Q.op("pe", lambda e: e.matmul(ps[:, hh, c0:TT], lhsT=kT[hs, buf, kb * 128:(kb + 1) * 128], rhs=qT[hs, buf, ql],
                                              start=True, stop=True),
                     reads=["kT%d" % buf, "qT%d" % buf], writes=["Z"], inc=(hh == 1))

        def st_E(u):
            jh, qt, kb = units[u]
            c0 = c0_of(u)
            e4 = u % NE
            Q.op("act", lambda e: e.activation(out=Eb[:, e4, :, c0:TT], in_=ps[:, 0:2, c0:TT], func=AF.Exp, scale=0.125),
                 reads=["Z"], writes=["E%d" % e4])

        def st_M(u):
            jh, qt, kb = units[u]
            if kb < 4 * qt:
                return
            c0 = c0_of(u)
            e4 = u % NE
            Q.op("dve", lambda e: e.tensor_tensor(out=Eb[:, e4, :, c0:c0 + 128], in0=Eb[:, e4, :, c0:c0 + 128],
                                                  in1=SL[:, None, :].to_broadcast([128, 2, 128]), op=ALU.mult),
                 reads=["E%d" % e4, "cst"], writes=["E%d" % e4])

        def st_L(u):
            c0 = c0_of(u)
            e4, l3 = u % NE, u % NL
            Q.op("act", lambda e: e.activation(out=Lb[:, l3, :, c0:TT], in_=Eb[:, e4, :, c0:TT], func=AF.Ln, bias=1.0, scale=1.0),
                 reads=["E%d" % e4], writes=["L%d" % l3])

        def st_mm1(u, hh):
            c0 = c0_of(u)
            l3 = u % NL
            Q.op("pe", lambda e: e.matmul(ps[:, 2 + hh, c0:TT], lhsT=U, rhs=Lb[:, l3, hh, c0:TT], start=first(u), stop=True,
                                          skip_group_check=True),
                 reads=["L%d" % l3, "cst"], writes=["G%d" % hh], inc=True)

        def st_P(u):
            c0 = c0_of(u)
            p2 = u % 2
            if c0 >= 256:
                Q.op("act", lambda e: e.activation(out=Pb[:, p2, :, c0:TT], in_=ps[:, 2:4, c0:TT], func=AF.Exp, scale=-1.0),
                     reads=["G0", "G1"], writes=["P%dh0" % p2, "P%dh1" % p2])
                return
            for hh in range(2):
                Q.op("act", lambda e: e.activation(out=Pb[:, p2, hh, c0:TT], in_=ps[:, 2 + hh, c0:TT], func=AF.Exp, scale=-1.0),
                     reads=["G%d" % hh], writes=["P%dh%d" % (p2, hh)])

        def st_mm2(u, hh):
            if last(u):
                return
            c0 = c0_of(u)
            l3 = u % NL
            Q.op("pe", lambda e: e.matmul(ps[:, 2 + hh, c0:TT], lhsT=SL, rhs=Lb[:, l3, hh, c0:TT], start=False, stop=True,
                                          skip_group_check=True),
                 reads=["L%d" % l3, "cst"], writes=["G%d" % hh], inc=False)

        def st_A(u):
            c0 = c0_of(u)
            p2, e4 = u % 2, u % NE
            if c0 >= 256:
                Q.op("dve", lambda e: e.tensor_tensor(out=Ab[:, p2, :, c0:TT], in0=Eb[:, e4, :, c0:TT], in1=Pb[:, p2, :, c0:TT], op=ALU.mult),
                     reads=["E%d" % e4, "P%dh0" % p2, "P%dh1" % p2], writes=["A%dh0" % p2, "A%dh1" % p2])
            else:
                for hh in range(2):
                    Q.op("dve", lambda e: e.tensor_tensor(out=Ab[:, p2, hh, c0:TT], in0=Eb[:, e4, hh, c0:TT], in1=Pb[:, p2, hh, c0:TT],
                                                          op=ALU.mult),
                         reads=["E%d" % e4, "P%dh%d" % (p2, hh)], writes=["A%dh%d" % (p2, hh)])

        def st_AV(u, step):
            jh, qt, kb = units[u]
            buf = jh % 2
            c0 = c0_of(u)
            p2 = u % 2
            ob = 4
            for hh in range(2):
                hs = slice(hh * 64, (hh + 1) * 64)
                Q.op("pe", lambda e: e.matmul(ps[hs, ob, c0:TT], lhsT=Vt[:, buf, kb, hs], rhs=Ab[:, p2, hh, c0:TT], start=first(u), stop=True,
                                              skip_group_check=True),
                     reads=["Vt%d" % buf, "A%dh%d" % (p2, hh)], writes=["O%d" % ob], inc=(hh == 1))
            if last(u):
                r = chain_i[0] % 2
                tl = slice(qt * TT, (qt + 1) * TT)
                Q.op("dve", lambda e: e.tensor_tensor(out=ozb[:, r, :], in0=ps[:, ob, :], in1=zg[:, buf, tl], op=ALU.mult),
                     reads=["O%d" % ob, "zg%d" % buf], writes=["oz%d" % r])
                new = [[step, it_] for it_ in outproj_items(jh, qt, r)]
                if jh == 7 and not debug:
                    new.append([step, final_item(qt, [e_[1] for e_ in new])])
                if qt == 3 and jh + 2 < 8:
                    new.append([step, Item([lambda: wload(wB_d, wBo_d, jh + 2, jh % 2)], cost=0.0)])
                    new.append([step + 10, pair_prep_item(jh + 2)])
                    new += [[step + 10, it_] for it_ in proj_items(jh + 2)]
                k = 0
                while k < len(bg) and bg[k][0] <= step and bg[k][1].is_out:
                    k += 1
                bg[k:k] = new
                chain_i[0] += 1

        items0 = proj_items(0)
        bg.append([-100, pair_prep_item(0)])
        bg += [[-100, it_] for it_ in items0]
        if debug:
            bg_flush_all()

        def ensure(items, step):
            guard = 0
            while not all(i_.done for i_ in items):
                bg_advance()
                bg_start(step, 100.0, force=True)
                guard += 1
                assert guard < 1000
        dbg_dump("B0", [kT[:, 0, :].bitcast(F32), qT[:, 0, :].bitcast(F32), zg[:, 0, :].bitcast(F32),
                        Vt[:, 0, :, :].rearrange("p a b -> p (a b)").bitcast(F32)],
                 [out_d[:, 0, 0:1024], out_d[:, 1, 0:1024], out_d[:, 2, 0:1024], out_d[:, 3, 0:1024]], [])
        bg.append([6, pair_prep_item(1)])
        bg += [[6, it_] for it_ in proj_items(1)]

        for s in range(-4, n):
            if s + 4 < n and units[s + 4][1] == 0 and units[s + 4][2] == 3 and units[s + 4][0] > 0:
                bg_flush(s)
            if 0 <= s < n:
                st_P(s)
            if 0 <= s + 3 < n:
                st_E(s + 3)
            if 0 <= s + 2 < n:
                st_L(s + 2)
            if 0 <= s < n:
                st_A(s)
            if 0 <= s + 3 < n:
                st_M(s + 3)
            for hh in range(2):
                if 0 <= s < n:
                    st_mm2(s, hh)
                if 0 <= s + 1 < n:
                    st_mm1(s + 1, hh)
            if 0 <= s + 4 < n:
                if units[s + 4][0] == 0 and not debug:
                    ensure([items0[4 * (units[s + 4][2] // 4)], items0[4 * units[s + 4][1] + 2]], s)
                st_QK(s + 4)
            if 0 <= s < n:
                if units[s][0] == 0 and not debug:
                    need = [items0[4 * (units[s][2] // 4) + 1]]
                    if last(s):
                        need.append(items0[4 * units[s][1] + 3])
                    ensure(need, s)
                st_AV(s, s)
            bg_advance()
            u_ref = min(max(s + 2, 0), n - 1)
            bg_start(s, 2.0 if c0_of(u_ref) == 0 else 0.5)
        bg_flush_all()
        if debug == "B1":
            dbg_dump("B1", xT[:], out_d, [])

        dbg_dump("B", xT[:], out_d, [("x", c, t) for c in range(8) for t in range(4)])
        Q.wait_all("sp", keys=["dma_out"])
        Q.emit(nc, st)
    except _Stop:
        pass
    return nc


_DEBUG = False
ZPOOL = False
FINAL_BARRIER = False


def _chunkT(v):
    v = np.asarray(v, np.float32)
    lead = v.shape[:-1]
    return np.ascontiguousarray(np.moveaxis(v.reshape(lead + (8, 128)), -1, 0))


def _prep_shared(a_mod_w, a_mod_b, a_norm_g, a_w_in, a_conv_w, a_w_out, kv_mod_w, kv_mod_b, kv_norm_g, w_kv,
                 b_mod_w, b_mod_b, b_norm_g, b_w_qz, b_w_out, final_norm_g):
    f = lambda a: np.asarray(a, np.float32)
    modW = np.concatenate([f(a_mod_w)[0], f(kv_mod_w), f(b_mod_w)[0]], axis=1)
    modT = np.ascontiguousarray(modW.T.reshape(64, 128, D)[0:24])
    w2 = modW[:, 3072:].reshape(8, 128, 10, 512)
    modW2 = np.ascontiguousarray(w2.transpose(2, 1, 0, 3))
    modb = np.concatenate([f(a_mod_b)[0], f(kv_mod_b), f(b_mod_b)[0]])
    vecs = np.zeros((128, 120), np.float32)
    vecs[:, 0:64] = modb.reshape(64, 128).T
    g4 = np.stack([f(a_norm_g)[0], f(kv_norm_g), f(b_norm_g)[0], f(final_norm_g)])
    vecs[:, 64:96] = g4.reshape(4, 8, 128).transpose(2, 0, 1).reshape(128, 32)
    cw = f(a_conv_w)[0]
    vecs[:, 96:120] = cw.reshape(3, 8, 128).transpose(2, 0, 1).reshape(128, 24)
    win = f(a_w_in)[0].reshape(8, 128, 4, 8, 128)
    wA = np.ascontiguousarray(win.transpose(3, 1, 0, 2, 4).reshape(8, 128, 8, 512))
    wAo = np.ascontiguousarray(f(a_w_out)[0].reshape(8, 128, D))
    wkv = f(w_kv).reshape(8, 128, 2, 8, 128)
    wqz = f(b_w_qz)[0].reshape(8, 128, 2, 8, 128)
    wcat = np.concatenate([wkv, wqz], axis=2)
    wB = np.ascontiguousarray(wcat.transpose(3, 1, 0, 2, 4).reshape(8, 128, 8, 512))
    wBo = np.ascontiguousarray(f(b_w_out)[0].reshape(8, 128, D))
    return dict(modT=modT, modW2=modW2, vecs=vecs, wA=wA, wAo=wAo, wB=wB, wBo=wBo)


def kernel(x, c, a_mod_w, a_mod_b, a_norm_g, a_w_in, a_conv_w, a_w_out, kv_mod_w, kv_mod_b, kv_norm_g, w_kv,
           b_mod_w, b_mod_b, b_norm_g, b_w_qz, b_w_out, final_norm_g):
    x = np.asarray(x, np.float32)
    c = np.asarray(c, np.float32)
    shared = _prep_shared(a_mod_w, a_mod_b, a_norm_g, a_w_in, a_conv_w, a_w_out, kv_mod_w, kv_mod_b, kv_norm_g, w_kv,
                          b_mod_w, b_mod_b, b_norm_g, b_w_qz, b_w_out, final_norm_g)
    in_maps = []
    for b in range(NB):
        m = dict(shared)
        m["xT"] = np.ascontiguousarray(x[b].T.reshape(8, 128, S).transpose(1, 0, 2))
        m["crep"] = np.ascontiguousarray(np.broadcast_to(c[b][None, :], (128, D)))
        m["cT"] = np.ascontiguousarray(c[b].reshape(8, 128).T)
        in_maps.append(m)
    nc = build_nc(debug=_DEBUG)
    res = run_bass_kernel_spmd(nc, in_maps, core_ids=list(range(NB)))
    out = np.empty((NB, S, D), np.float32)
    for b in range(NB):
        oT = np.asarray(res.results[b]["outT"], np.float32)
        out[b] = oT.transpose(2, 1, 0).reshape(S, D)
    return out
```

```python
import contextlib
import numpy as np
import concourse.bass as bass
import concourse.mybir as mybir
from concourse.bass_utils import run_bass_kernel_spmd

F32 = mybir.dt.float32
BF16 = mybir.dt.bfloat16
AF = mybir.ActivationFunctionType
ALU = mybir.AluOpType

D = 1024
S = 2048
NB = 8
NC_CH = 8
TT = 512
NTT = S // TT
EPS = 1e-6
FUSE_WAITS = True


class _Rec:
    def __init__(self):
        self.call = None

    def __getattr__(self, name):
        def f(*a, **k):
            self.call = (name, a, k)
            return None
        return f


class Sched:
    ENGS = ("pe", "act", "dve", "pool", "sp")

    def __init__(self):
        self.ops = {e: [] for e in self.ENGS}
        self.cnt = {}
        self.waited = {e: {} for e in self.ENGS}
        self.last_w = {}
        self.readers = {}
        self.pending_pe = False

    def _tok_after(self, key, amount):
        self.cnt[key] = self.cnt.get(key, 0) + amount
        return (key, self.cnt[key])

    def _add_wait(self, eng, waits, tok, prod_eng):
        if tok is None:
            return
        key, val = tok
        if prod_eng == "pe" and eng == "pe":
            return
        if self.waited[eng].get(key, 0) >= val:
            return
        waits[key] = max(waits.get(key, 0), val)

    def op(self, eng, fn, reads=(), writes=(), inc=True, dma=None):
        waits = {}
        for r in reads:
            t = self.last_w.get(r)
            if t is not None:
                self._add_wait(eng, waits, t[0], t[1])
        for w in writes:
            t = self.last_w.get(w)
            if t is not None:
                self._add_wait(eng, waits, t[0], t[1])
            for t in self.readers.get(w, ()):
                self._add_wait(eng, waits, t[0], t[1])
        for k, v in waits.items():
            self.waited[eng][k] = v
        if dma is not None:
            key = "dma_" + dma
            tok = self._tok_after(key, 16)
            prod = "dma"
            incspec = (key, 16)
        else:
            key = eng
            if inc:
                tok = self._tok_after(key, 1)
                incspec = (key, 1)
            else:
                assert eng == "pe"
                tok = (key, self.cnt.get(key, 0) + 1)
                incspec = None
            prod = eng
        rec = _Rec()
        fn(rec)
        assert rec.call is not None
        self.ops[eng].append((sorted(waits.items()), rec.call, incspec))
        for w in writes:
            self.last_w[w] = (tok, prod)
            self.readers[w] = []
        for r in reads:
            self.readers.setdefault(r, []).append((tok, prod))
        return tok

    def wait_all(self, eng, keys=None):
        waits = {}
        for k, v in self.cnt.items():
            if keys is not None and k not in keys:
                continue
            if self.waited[eng].get(k, 0) < v:
                waits[k] = v
                self.waited[eng][k] = v
        self.ops[eng].append((sorted(waits.items()), None, None))

    def emit(self, nc, stack):
        sems = {}
        for k in self.cnt:
            sems[k] = stack.enter_context(nc.semaphore("s_" + k))
        block = stack.enter_context(nc.Block())

        FUSABLE = ("activation", "tensor_tensor", "tensor_scalar", "scalar_tensor_tensor", "tensor_copy", "reciprocal")

        def run(e, name):
            for waits, fn, incspec in self.ops[name]:
                fused = None
                if (FUSE_WAITS and fn is not None and waits and name in ("act", "dve", "pool") and fn[0] in FUSABLE
                        and fn[2].get("accum_out") is None):
                    fused = waits[-1]
                    waits = waits[:-1]
                for k, v in waits:
                    e.wait_ge(sems[k], v)
                if fn is None:
                    continue
                name_, a_, k_ = fn
                ins = getattr(e, name_)(*a_, **k_)
                if fused is not None:
                    ins._wait_ge(sems[fused[0]], fused[1])
                if incspec is not None:
                    ins.then_inc(sems[incspec[0]], incspec[1])

        @block.tensor
        def _(e):
            run(e, "pe")

        @block.scalar
        def _(e):
            run(e, "act")

        @block.vector
        def _(e):
            run(e, "dve")

        @block.gpsimd
        def _(e):
            run(e, "pool")

        @block.sync
        def _(e):
            run(e, "sp")


class _Stop(Exception):
    pass


class Carve:
    def __init__(self, t):
        self.t = t
        self.n = t.shape[1]
        self.off = 0

    def reset(self):
        self.off = 0

    def alloc(self, shape, dt):
        n = int(np.prod(shape))
        n16 = n * (2 if dt == F32 else 1)
        n16 = (n16 + 15) // 16 * 16
        assert self.off + n16 <= self.n, ("scratch overflow", self.off, n16, self.n)
        ap = self.t[:, self.off:self.off + n16]
        self.off += n16
        if dt == F32:
            ap = ap.bitcast(F32)
        ap = ap[:, 0:n]
        if len(shape) == 2:
            ap = ap.rearrange("p (a b) -> p a b", a=shape[0])
        elif len(shape) == 3:
            ap = ap.rearrange("p (a b c) -> p a b c", a=shape[0], b=shape[1])
        return ap


def build_nc(debug=False):
    nc = bass.Bass("TRN2", target_bir_lowering=False)
    dram = lambda name, shape, kind: nc.dram_tensor(name, shape, F32, kind=kind).ap()
    xT_d = dram("xT", [128, 8, S], "ExternalInput")
    crep_d = dram("crep", [128, D], "ExternalInput")
    modT_d = dram("modT", [24, 128, D], "ExternalInput")
    modW2_d = dram("modW2", [10, 128, 8, 512], "ExternalInput")
    cT_d = dram("cT", [128, 8], "ExternalInput")
    vecs_d = dram("vecs", [128, 120], "ExternalInput")
    wA_d = dram("wA", [8, 128, 8, 512], "ExternalInput")
    wAo_d = dram("wAo", [8, 128, D], "ExternalInput")
    wB_d = dram("wB", [8, 128, 8, 512], "ExternalInput")
    wBo_d = dram("wBo", [8, 128, D], "ExternalInput")
    out_d = dram("outT", [128, 8, S], "ExternalOutput")

    Q = Sched()
    try:
      with contextlib.ExitStack() as st:
        sb = lambda name, shape, dt: st.enter_context(nc.sbuf_tensor(name, shape, dt))
        xT = sb("xT_sb", [128, 8, S], F32)
        xn = sb("xn_sb", [128, 8, S], BF16)
        crep = sb("crep_sb", [128, D], F32)
        vecs = sb("vecs_sb", [128, 120], F32)
        mod = sb("mod_sb", [128, 64], F32)
        der = sb("der_sb", [128, 64], F32)
        derb = sb("derb_sb", [128, 32], BF16)
        shbc = sb("shbc_sb", [128, 8, 128], BF16)
        cTs = sb("cT_sb", [128, 8], F32)
        scb = sb("scb_sb", [128, 8], BF16)
        NSTG = 4
        wch = sb("wch_sb", [128, 2, 8, 512], BF16)
        wo = sb("wo_sb", [128, 2, D], BF16)
        cst = sb("cst_sb", [128, 3, 128], BF16)
        sq = sb("sq_sb", [128, 3, TT], BF16)
        rstd = sb("rstd_sb", [128, 2, TT], F32)
        lnt = sb("lnt_sb", [128, TT], F32)
        scratch = sb("scratch_sb", [128, 38000], BF16)
        ps = st.enter_context(nc.psum_tensor("ps", [128, 8, 512], F32))
        cv = Carve(scratch)
        stage_raw = cv.alloc([1, NSTG * D * 2], BF16)[:, 0, :]
        stage = stage_raw.bitcast(F32).rearrange("p (a b) -> p a b", a=NSTG)
        modw = stage_raw.rearrange("p (a k n) -> p a k n", a=2, k=8)
        junk = cv.alloc([1, D], F32)[:, 0, :]
        cu = cv.alloc([2, S + 2], F32)
        csb = cv.alloc([2, TT], F32)
        szb = cv.alloc([2, TT], F32)
        acc = cv.alloc([2, TT], F32)
        yj = cv.alloc([4, TT], BF16)
        bias4 = cv.alloc([4, 4], F32)
        wchX = cv.alloc([2, 8, 512], BF16)
        woX = cv.alloc([2, D], BF16)

        def WCH(k):
            return wch[:, k] if k < 2 else wchX[:, k - 2]

        def WO(k):
            return wo[:, k] if k < 2 else woX[:, k - 2]

        U = cst[:, 0, :]
        SL = cst[:, 1, :]
        ONES = cst[:, 2, :]

        def dbg_dump(tag, src_ap, dst_ap, reads):
            if debug != tag:
                return
            for eng in Sched.ENGS:
                Q.wait_all(eng)
            if not isinstance(src_ap, list):
                src_ap, dst_ap = [src_ap], [dst_ap]
            for sa, da in zip(src_ap, dst_ap):
                Q.op("sp", lambda e, sa=sa, da=da: e.dma_start(out=da, in_=sa), reads=reads, dma="out")
            Q.wait_all("sp", keys=["dma_out"])
            Q.emit(nc, st)
            raise _Stop()

        GMA, GMKV, GMB, NBZ = 0, 8, 16, 24

        Q.op("pool", lambda e: e.memset(cst[:], 1.0), writes=["cst"])
        Q.op("pool", lambda e: e.affine_select(out=U, in_=U, pattern=[[-1, 128]], compare_op=ALU.is_ge,
                                               fill=0.0, base=0, channel_multiplier=1), reads=["cst"], writes=["cst"])
        Q.op("pool", lambda e: e.affine_select(out=SL, in_=SL, pattern=[[1, 128]], compare_op=ALU.is_gt,
                                               fill=0.0, base=0, channel_multiplier=-1), reads=["cst"], writes=["cst"])

        Q.op("sp", lambda e: e.dma_start(out=crep[:], in_=crep_d), writes=["crep"], dma="c")
        Q.op("sp", lambda e: e.dma_start(out=vecs[:], in_=vecs_d), writes=["vecs"], dma="v")

        mod_next = [0]

        def mod_dma():
            j = mod_next[0]
            if j >= 24:
                return
            if j == 16:
                for t_ in range(1, NTT):
                    x_dma(t_)
            mod_next[0] += 1
            Q.op("sp", lambda e, j=j: e.dma_start(out=stage[:, j % NSTG, :], in_=modT_d[j]),
                 writes=["stage%d" % (j % NSTG)], dma="m%d" % (j % NSTG))

        mod_done = [0]

        def mod_reduce():
            j = mod_done[0]
            if j >= 24:
                return
            mod_done[0] += 1
            if j < 64:
                Q.op("dve", lambda e, j=j: e.scalar_tensor_tensor(
                    out=junk[:], in0=stage[:, j % NSTG, :], scalar=1.0, in1=crep[:], op0=ALU.mult, op1=ALU.mult,
                    accum_out=mod[:, j:j + 1]), reads=["stage%d" % (j % NSTG), "crep"], writes=["junk", "mod%d" % (j // 8)])
            else:
                Q.op("pool", lambda e: e.tensor_tensor(out=junk2[:], in0=stage[:, j % NSTG, :], in1=crep[:], op=ALU.mult),
                     reads=["stage%d" % (j % NSTG), "crep"], writes=["junk2"])
                Q.op("pool", lambda e: e.tensor_reduce(out=mod[:, j:j + 1], in_=junk2[:], axis=mybir.AxisListType.X, op=ALU.add),
                     reads=["junk2"], writes=["mod%d" % (j // 8)])
            mod_dma()

        def x_dma(t):
            tl_ = slice(t * TT, (t + 1) * TT)
            Q.op("sp", lambda e: e.dma_start(out=xT[:, :, tl_], in_=xT_d[:, :, tl_]),
                 writes=[("x", c, t) for c in range(8)], dma="x%d" % t)

        x_dma(0)
        for _ in range(NSTG):
            mod_dma()

        def wload(src_d, srco_d, j, buf):
            for h in range(2):
                Q.op("pool", lambda e, h=h: e.dma_start(out=WCH(buf)[:, 4 * h:4 * h + 4, :], in_=src_d[j, :, 4 * h:4 * h + 4, :]),
                     writes=["wch%d" % buf], dma="wc%d" % buf)
            Q.op("pool", lambda e: e.dma_start(out=WO(buf), in_=srco_d[j]), writes=["wo%d" % buf], dma="wo%d" % buf)

        for j_ in range(2):
            wload(wA_d, wAo_d, j_, j_)

        Q.op("sp", lambda e: e.dma_start(out=cTs[:], in_=cT_d), writes=["cTs"], dma="ct")
        Q.op("act", lambda e: e.activation(out=scb[:], in_=cTs[:], func=AF.Silu), reads=["cTs"], writes=["scb"])

        m2_dma_i = [0]
        m2_pe_i = [0]

        def m2_dma():
            i = m2_dma_i[0]
            if i >= 10:
                return
            m2_dma_i[0] += 1
            a = i % 2
            for h in range(2):
                Q.op("pool", lambda e: e.dma_start(out=modw[:, a, 4 * h:4 * h + 4, :], in_=modW2_d[i, :, 4 * h:4 * h + 4, :]),
                     writes=["stage%d" % (2 * a), "stage%d" % (2 * a + 1)], dma="mw%d" % a)

        def m2_compute():
            i = m2_pe_i[0]
            if i >= 10:
                return
            m2_pe_i[0] += 1
            a = i % 2
            for g in range(4):
                for kc in range(8):
                    Q.op("pe", lambda e: e.matmul(ps[:, 7, 8 + g:9 + g], lhsT=modw[:, a, kc, g * 128:(g + 1) * 128], rhs=scb[:, kc:kc + 1],
                                                  start=(kc == 0), stop=(kc == 7)),
                         reads=["stage%d" % (2 * a), "stage%d" % (2 * a + 1), "scb"], writes=["ps7"], inc=(kc == 7 and g == 3))
            j0 = 24 + 4 * i
            Q.op("dve", lambda e: e.tensor_tensor(out=mod[:, j0:j0 + 4], in0=ps[:, 7, 8:12], in1=vecs[:, j0:j0 + 4], op=ALU.add),
                 reads=["ps7", "vecs"], writes=["mod%d" % (j0 // 8)])
            m2_dma()

        Q.op("act", lambda e: e.activation(out=crep[:], in_=crep[:], func=AF.Silu), reads=["crep"], writes=["crep"])

        def mod_finish(g0, g1):
            Q.op("dve", lambda e: e.tensor_tensor(out=mod[:, 8 * g0:8 * g1], in0=mod[:, 8 * g0:8 * g1],
                                                  in1=vecs[:, 8 * g0:8 * g1], op=ALU.add),
                 reads=["mod%d" % g for g in range(g0, g1)] + ["vecs"], writes=["mod%d" % g for g in range(g0, g1)])

        def gmod(dst, scale_col, gwhich, grp):
            Q.op("dve", lambda e: e.scalar_tensor_tensor(out=der[:, dst:dst + 8], in0=mod[:, scale_col:scale_col + 8], scalar=1.0,
                                                         in1=vecs[:, 64 + 8 * gwhich:72 + 8 * gwhich], op0=ALU.add, op1=ALU.mult),
                 reads=["mod%d" % grp, "vecs"], writes=["der%d" % (dst // 8)])

        def shiftbf(dst, shift_col, grp):
            Q.op("dve", lambda e: e.tensor_copy(out=derb[:, dst:dst + 8], in_=mod[:, shift_col:shift_col + 8]),
                 reads=["mod%d" % grp], writes=["derb%d" % (dst // 8)])

        for _ in range(16):
            mod_reduce()
        mod_finish(0, 2)
        gmod(GMA, 8, 0, 1)
        shiftbf(0, 0, 0)
        dbg_dump("mod", mod[:], out_d[:, 0, 0:64], ["mod0", "mod1", "mod2"])

        stat_i = [0]

        def stats(tt, bank):
            for c in range(8):
                k = stat_i[0] % 3
                stat_i[0] += 1
                Q.op("act", lambda e, c=c, k=k: e.activation(out=sq[:, k, :], in_=xT[:, c, tt * TT:(tt + 1) * TT], func=AF.Square),
                     reads=[("x", c, tt)], writes=["sq%d" % k])
                Q.op("pe", lambda e, c=c, k=k: e.matmul(ps[:, bank, :], lhsT=ONES, rhs=sq[:, k, :], start=(c == 0), stop=(c == 7)),
                     reads=["sq%d" % k, "cst"], writes=["ps%d" % bank], inc=True)
            r = tt % 2
            Q.op("act", lambda e: e.activation(out=lnt[:], in_=ps[:, bank, :], func=AF.Ln, scale=1.0 / D, bias=EPS),
                 reads=["ps%d" % bank], writes=["lnt"])
            Q.op("act", lambda e: e.activation(out=rstd[:, r, :], in_=lnt[:], func=AF.Exp, scale=-0.5),
                 reads=["lnt"], writes=["rstd%d" % r])
            return r

        def normalize(tt, bank):
            r = stats(tt, bank)
            tl_ = slice(tt * TT, (tt + 1) * TT)
            for c4 in range(2):
                cs_ = slice(4 * c4, 4 * c4 + 4)
                Q.op("dve", lambda e: e.tensor_tensor(out=xn[:, cs_, tl_], in0=xT[:, cs_, tl_],
                                                      in1=rstd[:, r, None, :].to_broadcast([128, 4, TT]), op=ALU.mult),
                     reads=[("x", c, tt) for c in range(4 * c4, 4 * c4 + 4)] + ["rstd%d" % r], writes=[("xn", tt)])

        normalize(0, 7)
        if debug == "xn":
            for tt in range(1, NTT):
                normalize(tt, 7)
        if debug == "xn":
            Q.op("dve", lambda e: e.tensor_copy(out=xT[:], in_=xn[:]), reads=[("xn", t) for t in range(4)],
                 writes=[("x", c, t) for c in range(8) for t in range(4)])
        dbg_dump("xn", xT[:], out_d, [("x", c, t) for c in range(8) for t in range(4)])

        for b in range(2):
            Q.op("dve", lambda e, b=b: e.memset(cu[:, b, 0:2], 0.0), writes=["cu%d" % b])

        def chunk_prep(buf, groups, pbank):
            ng = (groups[-1][1]) // 128
            for g in range(ng):
                sc = [s for (c0, c1, s, gm) in groups if c0 <= g * 128 < c1][0]
                for kc in range(8):
                    Q.op("pe", lambda e, g=g, kc=kc, sc=sc: e.matmul(ps[:, pbank, g:g + 1], lhsT=WCH(buf)[:, kc, g * 128:(g + 1) * 128],
                                                                    rhs=derb[:, sc + kc:sc + kc + 1], start=(kc == 0), stop=(kc == 7)),
                         reads=["wch%d" % buf, "derb%d" % (sc // 8)], writes=["ps%d" % pbank], inc=(kc == 7 and g == ng - 1))

        def chunk_fold(buf, groups):
            for (c0, c1, s, gm) in groups:
                for kc in range(8):
                    Q.op("dve", lambda e, kc=kc, c0=c0, c1=c1, gm=gm: e.tensor_scalar(
                        out=WCH(buf)[:, kc, c0:c1], in0=WCH(buf)[:, kc, c0:c1], scalar1=der[:, gm + kc:gm + kc + 1], scalar2=None,
                        op0=ALU.mult), reads=["wch%d" % buf, "der%d" % (gm // 8)], writes=["wch%d" % buf])

        it = 0
        GORDER = [3, 1, 2, 0]
        pend = [None]

        def outproj_A(pjp, ptt, pr0, pr1, jos=range(8), tail=True):
            ptl = slice(ptt * TT, (ptt + 1) * TT)
            b0, b1 = (2 * pjp) % 4, (2 * pjp + 1) % 4
            for jo in jos:
                ob = 6 + (jo % 2)
                Q.op("pe", lambda e: e.matmul(ps[:, ob, :], lhsT=WO(b0)[:, jo * 128:(jo + 1) * 128], rhs=yj[:, pr0, :],
                                              start=True, stop=False),
                     reads=["wo%d" % b0, "yj%d" % pr0], writes=["ps%d" % ob], inc=False)
                Q.op("pe", lambda e: e.matmul(ps[:, ob, :], lhsT=WO(b1)[:, jo * 128:(jo + 1) * 128], rhs=yj[:, pr1, :],
                                              start=False, stop=True),
                     reads=["wo%d" % b1, "yj%d" % pr1], writes=["ps%d" % ob])
                Q.op("dve", lambda e: e.scalar_tensor_tensor(out=xT[:, jo, ptl], in0=ps[:, ob, :], scalar=mod[:, 16 + jo:17 + jo],
                                                             in1=xT[:, jo, ptl], op0=ALU.mult, op1=ALU.add),
                     reads=["ps%d" % ob, "mod2", ("x", jo, ptt)], writes=[("x", jo, ptt)])
            if tail and ptt == NTT - 1:
                for jj_ in range(2):
                    pj = 2 * pjp + jj_
                    if pj + 4 < 8:
                        wload(wA_d, wAo_d, pj + 4, pj % 4)
                    elif pjp == 2:
                        wload(wB_d, wBo_d, jj_, jj_)

        grpA = [(0, 512, 0, GMA)]

        def prep_A(jp_):
            for jj_ in range(2):
                buf_ = (2 * jp_ + jj_) % 4
                chunk_prep(buf_, grpA, 7)
                Q.op("dve", lambda e: e.tensor_copy(out=bias4[:, buf_, :], in_=ps[:, 7, 0:4]), reads=["ps7"], writes=["bias4%d" % buf_])
                chunk_fold(buf_, grpA)

        prep_A(0)
        for jp in range(4):
            for tt in range(NTT):
                tl = slice(tt * TT, (tt + 1) * TT)
                rr = []
                for jj in range(2):
                    j = 2 * jp + jj
                    buf = j % 4
                    cb = j % 2
                    if jp == 0 and jj == 1 and tt + 1 < NTT and debug != "xn":
                        normalize(tt + 1, 7)
                    banks = [(4 * it + g) % 6 for g in range(4)]
                    for gi, g in enumerate(GORDER):
                        for kc in range(8):
                            Q.op("pe", lambda e: e.matmul(ps[:, banks[g], :], lhsT=WCH(buf)[:, kc, g * 128:(g + 1) * 128],
                                                          rhs=xn[:, kc, tl], start=(kc == 0), stop=(kc == 7)),
                                 reads=["wch%d" % buf, ("xn", tt)], writes=["ps%d" % banks[g]], inc=(kc == 7))
                        if jj == 0 and pend[0] is not None:
                            outproj_A(*pend[0], jos=(2 * gi, 2 * gi + 1), tail=(gi == 3))
                    r = it % 2
                    r4 = it % 4
                    rr.append(r4)
                    bb, bc, bu, bz = [bias4[:, buf, g:g + 1] for g in range(4)]
                    Q.op("act", lambda e: e.activation(out=szb[:, r, :], in_=ps[:, banks[3], :], func=AF.Silu, bias=bz, scale=1.0),
                         reads=["ps%d" % banks[3], "bias4%d" % buf], writes=["sz%d" % r])
                    Q.op("act", lambda e: e.activation(out=csb[:, r, :], in_=ps[:, banks[1], :], func=AF.Identity, bias=bc, scale=1.0),
                         reads=["ps%d" % banks[1], "bias4%d" % buf], writes=["csb%d" % r])
                    if jj == 0:
                        pend[0] = None
                    cut = cu[:, cb, 2 + tt * TT:2 + (tt + 1) * TT]
                    Q.op("dve", lambda e: e.scalar_tensor_tensor(out=cut, in0=ps[:, banks[2], :], scalar=bu, in1=csb[:, r, :],
                                                                 op0=ALU.add, op1=ALU.mult),
                         reads=["ps%d" % banks[2], "csb%d" % r, "bias4%d" % buf], writes=["cu%d" % cb])
                    w0, w1, w2 = [vecs[:, 96 + 8 * k + j:97 + 8 * k + j] for k in range(3)]
                    Q.op("act", lambda e: e.activation(out=acc[:, r, :], in_=cut, func=AF.Copy, scale=w2),
                         reads=["cu%d" % cb, "vecs"], writes=["acc%d" % r])
                    Q.op("dve", lambda e: e.scalar_tensor_tensor(out=acc[:, r, :], in0=cu[:, cb, 1 + tt * TT:1 + (tt + 1) * TT], scalar=w1,
                                                                 in1=acc[:, r, :], op0=ALU.mult, op1=ALU.add),
                         reads=["cu%d" % cb, "acc%d" % r, "vecs"], writes=["acc%d" % r])
                    Q.op("dve", lambda e: e.scalar_tensor_tensor(out=acc[:, r, :], in0=cu[:, cb, tt * TT:(tt + 1) * TT], scalar=w0,
                                                                 in1=acc[:, r, :], op0=ALU.mult, op1=ALU.add),
                         reads=["cu%d" % cb, "acc%d" % r, "vecs"], writes=["acc%d" % r])
                    Q.op("dve", lambda e: e.scalar_tensor_tensor(out=acc[:, r, :], in0=ps[:, banks[0], :], scalar=bb, in1=acc[:, r, :],
                                                                 op0=ALU.add, op1=ALU.mult),
                         reads=["ps%d" % banks[0], "acc%d" % r, "bias4%d" % buf], writes=["acc%d" % r])
                    Q.op("pool", lambda e: e.tensor_tensor(out=yj[:, r4, :], in0=acc[:, r, :], in1=szb[:, r, :], op=ALU.mult),
                         reads=["acc%d" % r, "sz%d" % r], writes=["yj%d" % r4])
                    if it < 2:
                        for _ in range(4):
                            mod_reduce()
                        if it == 1:
                            mod_finish(2, 3)
                            m2_dma()
                            m2_dma()
                        wload(wA_d, wAo_d, 2 + it, 2 + it)
                    elif it % 2 == 1:
                        m2_compute()
                    if tt == NTT - 1 and jj == 1 and jp + 1 < 4:
                        prep_A(jp + 1)
                    it += 1
                pend[0] = (jp, tt, rr[0], rr[1])
        outproj_A(*pend[0])
        dbg_dump("A", xT[:], out_d, [("x", c, t) for c in range(8) for t in range(4)])
        while m2_pe_i[0] < 10:
            m2_compute()
        gmod(GMKV, 32, 1, 4)
        gmod(GMB, 48, 2, 6)
        shiftbf(8, 24, 3)
        shiftbf(16, 40, 5)
        for kc in range(8):
            Q.op("dve", lambda e, kc=kc: e.tensor_copy(out=shbc[:, kc, :], in_=mod[:, 24 + kc:25 + kc].to_broadcast([128, 128])),
                 reads=["mod3"], writes=["shbc"])

        for tt in range(NTT):
            normalize(tt, 7)

        for eng in Sched.ENGS:
            Q.wait_all(eng)

        cv.reset()
        kT = cv.alloc([2, S], BF16)
        qT = cv.alloc([2, S], BF16)
        zg = cv.alloc([2, S], BF16)
        Vt = cv.alloc([2, 16, 128], BF16)
        ozb = cv.alloc([2, TT], BF16)
        Eb = cv.alloc([4, 2, TT], BF16)
        Lb = cv.alloc([3, 2, TT], BF16)
        Pb = cv.alloc([2, 2, TT], BF16)
        Ab = cv.alloc([2, 2, TT], BF16)
        ez = cv.alloc([4, TT], F32)
        vbias = cv.alloc([2, 128], F32)
        bias3 = cv.alloc([2, 4], F32)
        nbz = cv.alloc([2, 1], F32)
        zbuf = cv.alloc([4, TT], F32)
        neg1 = None

        prj_i = [0]

        def prj_bank():
            prj_i[0] += 1
            return 6 + (prj_i[0] % 2)

        class Item:
            def __init__(self, stages, cost=1.7, barrier=False, is_out=False, nbanks=1, release_stage=1, after=()):
                self.stages = stages
                self.cost = cost
                self.barrier = barrier
                self.is_out = is_out
                self.nbanks = nbanks
                self.banks = []
                self.release_stage = release_stage
                self.after = list(after)
                self.done = False

        def pair_prep_item(jh):
            buf = jh % 2
            grpB = [(0, 256, 8, GMKV), (256, 512, 16, GMB)]
            st = {}

            def s0():
                st["pb"], st["pv"] = st["item"].banks
                chunk_prep(buf, grpB, st["pb"])
                pv = st["pv"]
                for kc in range(8):
                    Q.op("pe", lambda e: e.matmul(ps[:, pv, 0:128], lhsT=shbc[:, kc, :], rhs=wch[:, buf, kc, 128:256],
                                                  start=(kc == 0), stop=(kc == 7)),
                         reads=["shbc", "wch%d" % buf], writes=["ps%d" % pv], inc=(kc == 7))

            def s1():
                pb, pv = st["pb"], st["pv"]
                Q.op("dve", lambda e: e.tensor_copy(out=bias3[:, buf, :], in_=ps[:, pb, 0:4]), reads=["ps%d" % pb], writes=["bias3%d" % buf])
                Q.op("dve", lambda e: e.tensor_scalar(out=nbz[:, buf, :], in0=bias3[:, buf, 3:4], scalar1=-1.0, scalar2=None, op0=ALU.mult),
                     reads=["bias3%d" % buf], writes=["nbz%d" % buf])
                Q.op("dve", lambda e: e.tensor_copy(out=vbias[:, buf, :], in_=ps[:, pv, 0:128]), reads=["ps%d" % pv], writes=["vbias%d" % buf])

            def fold_stage(k0):
                def f():
                    for (c0, c1, s_, gm) in grpB:
                        for kc in range(k0, k0 + 2):
                            Q.op("dve", lambda e: e.tensor_scalar(out=wch[:, buf, kc, c0:c1], in0=wch[:, buf, kc, c0:c1],
                                                                  scalar1=der[:, gm + kc:gm + kc + 1], scalar2=None, op0=ALU.mult),
                                 reads=["wch%d" % buf, "der%d" % (gm // 8)], writes=["wch%d" % buf])
                return f

            st["item"] = Item([s0, s1] + [fold_stage(k) for k in (0, 2, 4, 6)] + [lambda: None], cost=0.8, barrier=True, nbanks=2)
            return st["item"]

        ztile_i = [0]

        def proj_items(jh):
            buf = jh % 2
            items = []

            def feat_tile(g, tt, kind):
                st = {}
                tl = slice(tt * TT, (tt + 1) * TT)
                bcol = bias3[:, buf, g:g + 1]

                def s0():
                    st["pb"] = st["item"].banks[0]
                    pb = st["pb"]
                    for kc in range(8):
                        Q.op("pe", lambda e: e.matmul(ps[:, pb, :], lhsT=wch[:, buf, kc, g * 128:(g + 1) * 128], rhs=xn[:, kc, tl],
                                                      start=(kc == 0), stop=(kc == 7)),
                             reads=["wch%d" % buf, ("xn", tt)], writes=["ps%d" % pb], inc=(kc == 7))

                def s1():
                    pb = st["pb"]
                    if kind == "k":
                        Q.op("dve", lambda e: e.tensor_scalar(out=kT[:, buf, tl], in0=ps[:, pb, :], scalar1=bcol, scalar2=None, op0=ALU.add),
                             reads=["ps%d" % pb, "bias3%d" % buf], writes=["kT%d" % buf])
                    elif kind == "q":
                        Q.op("dve", lambda e: e.tensor_scalar(out=qT[:, buf, tl], in0=ps[:, pb, :], scalar1=bcol, scalar2=None, op0=ALU.add),
                             reads=["ps%d" % pb, "bias3%d" % buf], writes=["qT%d" % buf])
                    else:
                        st["r"] = ztile_i[0] % 4
                        ztile_i[0] += 1
                        r = st["r"]
                        Q.op("act", lambda e: e.activation(out=ez[:, r, :], in_=ps[:, pb, :], func=AF.Exp, scale=-1.0, bias=nbz[:, buf, :]),
                             reads=["ps%d" % pb, "nbz%d" % buf], writes=["ez%d" % r])
                        Q.op("dve", lambda e: e.tensor_scalar(out=zbuf[:, r, :], in0=ps[:, pb, :], scalar1=bcol, scalar2=None, op0=ALU.add),
                             reads=["ps%d" % pb, "bias3%d" % buf, "ez%d" % r], writes=["zbuf%d" % r])

                def s2():
                    r = st["r"]
                    if ZPOOL:
                        Q.op("pool", lambda e: e.tensor_scalar(out=ez[:, r, :], in0=ez[:, r, :], scalar1=1.0, scalar2=None, op0=ALU.add),
                             reads=["ez%d" % r], writes=["ez%d" % r])
                        Q.op("pool", lambda e: e.tensor_tensor(out=ez[:, r, :], in0=ez[:, r, :], in1=neg1, op=ALU.pow),
                             reads=["ez%d" % r, "neg1"], writes=["ez%d" % r])
                        Q.op("pool", lambda e: e.tensor_tensor(out=zg[:, buf, tl], in0=zbuf[:, r, :], in1=ez[:, r, :], op=ALU.mult),
                             reads=["ez%d" % r, "zbuf%d" % r], writes=["zg%d" % buf])
                    else:
                        Q.op("dve", lambda e: e.tensor_scalar(out=ez[:, r, :], in0=ez[:, r, :], scalar1=1.0, scalar2=None, op0=ALU.add),
                             reads=["ez%d" % r], writes=["ez%d" % r])

                def rec_stage(q4):
                    def f():
                        r = st["r"]
                        cs = slice(q4 * 128, (q4 + 1) * 128)
                        Q.op("dve", lambda e: e.reciprocal(out=ez[:, r, cs], in_=ez[:, r, cs]), reads=["ez%d" % r], writes=["ez%d" % r])
                    return f

                def s7():
                    r = st["r"]
                    Q.op("dve", lambda e: e.tensor_tensor(out=zg[:, buf, tl], in0=zbuf[:, r, :], in1=ez[:, r, :], op=ALU.mult),
                         reads=["ez%d" % r, "zbuf%d" % r], writes=["zg%d" % buf])

                zst = [s2] + ([] if ZPOOL else [rec_stage(q4) for q4 in range(4)] + [s7])
                st["item"] = Item([s0, s1] + (zst if kind == "z" else []), cost=1.7)
                return st["item"]

            def v_tile(i4):
                st = {}

                def s0():
                    st["pb"] = st["item"].banks[0]
                    pb = st["pb"]
                    for ii in range(4):
                        i = i4 * 4 + ii
                        for kc in range(8):
                            Q.op("pe", lambda e: e.matmul(ps[:, pb, ii * 128:(ii + 1) * 128], lhsT=xn[:, kc, i * 128:(i + 1) * 128],
                                                          rhs=wch[:, buf, kc, 128:256], start=(kc == 0), stop=(kc == 7)),
                                 reads=["wch%d" % buf, ("xn", i // 4)], writes=["ps%d" % pb], inc=(kc == 7 and ii == 3))

                def s1():
                    pb = st["pb"]
                    Q.op("dve", lambda e: e.tensor_tensor(out=Vt[:, buf, i4 * 4:(i4 + 1) * 4, :],
                                                          in0=ps[:, pb, :].rearrange("p (a b) -> p a b", a=4),
                                                          in1=vbias[:, buf, None, :].to_broadcast([128, 4, 128]), op=ALU.add),
                         reads=["ps%d" % pb, "vbias%d" % buf], writes=["Vt%d" % buf])
                st["item"] = Item([s0, s1], cost=1.9)
                return st["item"]

            for tt in range(NTT):
                items.append(feat_tile(0, tt, "k"))
                items.append(v_tile(tt))
                items.append(feat_tile(2, tt, "q"))
                items.append(feat_tile(3, tt, "z"))
            return items

        def outproj_items(jh, qt, r):
            buf = jh % 2
            tl = slice(qt * TT, (qt + 1) * TT)
            items = []
            for jo in range(8):
                def mk(jo=jo):
                    st = {}

                    def s0():
                        pb = st["item"].banks[0]
                        Q.op("pe", lambda e: e.matmul(ps[:, pb, :], lhsT=wo[:, buf, jo * 128:(jo + 1) * 128], rhs=ozb[:, r, :],
                                                      start=True, stop=True),
                             reads=["wo%d" % buf, "oz%d" % r], writes=["ps%d" % pb])

                    def s1():
                        pb = st["item"].banks[0]
                        Q.op("dve", lambda e: e.scalar_tensor_tensor(out=xT[:, jo, tl], in0=ps[:, pb, :], scalar=mod[:, 56 + jo:57 + jo],
                                                                     in1=xT[:, jo, tl], op0=ALU.mult, op1=ALU.add),
                             reads=["ps%d" % pb, "mod7", ("x", jo, qt)], writes=[("x", jo, qt)])
                    st["item"] = Item([s0, s1], cost=0.25, is_out=True)
                    return st["item"]
                items.append(mk())
            return items

        def final_item(tt, deps):
            tl = slice(tt * TT, (tt + 1) * TT)
            st = {}
            r = tt % 2

            def sq_ops(cs):
                for c in cs:
                    k = stat_i[0] % 3
                    stat_i[0] += 1
                    st[c] = k
                    Q.op("dve", lambda e: e.tensor_tensor(out=sq[:, k, :], in0=xT[:, c, tl], in1=xT[:, c, tl], op=ALU.mult),
                         reads=[("x", c, tt)], writes=["sq%d" % k])

            def mm_ops(cs):
                bank = st["bank"]
                for c in cs:
                    k = st[c]
                    Q.op("pe", lambda e: e.matmul(ps[:, bank, :], lhsT=ONES, rhs=sq[:, k, :], start=(c == 0), stop=(c == 7)),
                         reads=["sq%d" % k, "cst"], writes=["ps%d" % bank], inc=True)

            def s0():
                st["bank"] = st["item"].banks[0]
                sq_ops((0, 1, 2))

            def s1():
                mm_ops((0, 1, 2))
                sq_ops((3, 4, 5))

            def s2():
                mm_ops((3, 4, 5))
                sq_ops((6, 7))

            def s3():
                mm_ops((6, 7))

            def s4():
                bank = st["bank"]
                Q.op("act", lambda e: e.activation(out=lnt[:], in_=ps[:, bank, :], func=AF.Ln, scale=1.0 / D, bias=EPS),
                     reads=["ps%d" % bank], writes=["lnt"])
                Q.op("act", lambda e: e.activation(out=rstd[:, r, :], in_=lnt[:], func=AF.Exp, scale=-0.5),
                     reads=["lnt"], writes=["rstd%d" % r])

            def stt(cs):
                def f():
                    for c in cs:
                        Q.op("dve", lambda e: e.scalar_tensor_tensor(out=xT[:, c, tl], in0=xT[:, c, tl], scalar=vecs[:, 88 + c:89 + c],
                                                                     in1=rstd[:, r, :], op0=ALU.mult, op1=ALU.mult),
                             reads=[("x", c, tt), "vecs", "rstd%d" % r], writes=[("x", c, tt)])
                return f

            def s7():
                stt((6, 7))()
                Q.op("sp", lambda e: e.dma_start(out=out_d[:, :, tl], in_=xT[:, :, tl]),
                     reads=[("x", c, tt) for c in range(8)], dma="out")

            st["item"] = Item([s0, s1, s2, s3, s4, stt((0, 1, 2)), stt((3, 4, 5)), s7], cost=0.5, release_stage=4, after=deps)
            return st["item"]

        NE, NL = 4, 3
        units = []
        for jh in range(8):
            for qt in range(4):
                for kb in range(4 * qt + 3, -1, -1):
                    units.append((jh, qt, kb))
        n = len(units)
        if debug == "B1":
            n = 40

        def c0_of(u):
            jh, qt, kb = units[u]
            o = kb - 4 * qt
            return 128 * o if o > 0 else 0

        def first(u):
            return units[u][2] == 4 * units[u][1] + 3

        def last(u):
            return units[u][2] == 0

        bg = []
        chain_i = [0]
        chain_of = {}

        inflight = []

        free_banks = [6, 7, 5]

        def bg_advance():
            for ent in list(inflight):
                ent[0].stages[ent[1]]()
                if ent[1] == ent[0].release_stage:
                    free_banks.extend(ent[0].banks)
                    ent[0].banks = []
                ent[1] += 1
                if ent[1] >= len(ent[0].stages):
                    ent[0].done = True
                    inflight.remove(ent)

        def bg_start(step, budget, force=False):
            while bg and (force or bg[0][0] <= step) and budget > 0:
                if any(ent[0].barrier for ent in inflight):
                    break
                item = bg[0][1]
                nb_ = item.nbanks if len(item.stages) > 1 else 0
                if len(free_banks) < nb_:
                    break
                if any(not d_.done for d_ in item.after):
                    break
                bg.pop(0)
                item.banks = [free_banks.pop(0) for _ in range(nb_)]
                budget -= item.cost
                item.stages[0]()
                if len(item.stages) > 1:
                    inflight.append([item, 1])
                else:
                    item.done = True
            return budget

        def bg_flush(step):
            while True:
                bg_advance()
                bg_start(step, 100.0, force=False)
                if not inflight and not (bg and bg[0][0] <= step):
                    break

        def bg_flush_all():
            while bg or inflight:
                bg_advance()
                bg_start(0, 100.0, force=True)

        def st_QK(u):
            jh, qt, kb = units[u]
            buf = jh % 2
            c0 = c0_of(u)
            ql = slice(qt * TT + c0, (qt + 1) * TT)
            for hh in range(2):
                hs = slice(hh * 64, (hh + 1) * 64)
                Q.op("pe", lambda e: e.matmul(ps[:, hh, c0:TT], lhsT=kT[hs, buf, kb * 128:(kb + 1) * 128], rhs=qT[hs, buf, ql],
                                              start=True, stop=True),
                     reads=["kT%d" % buf, "qT%d" % buf], writes=["Z"], inc=(hh == 1))

        def st_E(u):
            jh, qt, kb = units[u]
            c0 = c0_of(u)
            e4 = u % NE
            Q.op("act", lambda e: e.activation(out=Eb[:, e4, :, c0:TT], in_=ps[:, 0:2, c0:TT], func=AF.Exp, scale=0.125),
                 reads=["Z"], writes=["E%d" % e4])

        def st_M(u):
            jh, qt, kb = units[u]
            if kb < 4 * qt:
                return
            c0 = c0_of(u)
            e4 = u % NE
            Q.op("dve", lambda e: e.tensor_tensor(out=Eb[:, e4, :, c0:c0 + 128], in0=Eb[:, e4, :, c0:c0 + 128],
                                                  in1=SL[:, None, :].to_broadcast([128, 2, 128]), op=ALU.mult),
                 reads=["E%d" % e4, "cst"], writes=["E%d" % e4])

        def st_L(u):
            c0 = c0_of(u)
            e4, l3 = u % NE, u % NL
            Q.op("act", lambda e: e.activation(out=Lb[:, l3, :, c0:TT], in_=Eb[:, e4, :, c0:TT], func=AF.Ln, bias=1.0, scale=1.0),
                 reads=["E%d" % e4], writes=["L%d" % l3])

        def st_mm1(u, hh):
            c0 = c0_of(u)
            l3 = u % NL
            Q.op("pe", lambda e: e.matmul(ps[:, 2 + hh, c0:TT], lhsT=U, rhs=Lb[:, l3, hh, c0:TT], start=first(u), stop=True,
                                          skip_group_check=True),
                 reads=["L%d" % l3, "cst"], writes=["G%d" % hh], inc=True)

        def st_P(u):
            c0 = c0_of(u)
            p2 = u % 2
            if c0 >= 128:
                Q.op("act", lambda e: e.activation(out=Pb[:, p2, :, c0:TT], in_=ps[:, 2:4, c0:TT], func=AF.Exp, scale=-1.0),
                     reads=["G0", "G1"], writes=["P%dh0" % p2, "P%dh1" % p2])
                return
            for hh in range(2):
                Q.op("act", lambda e: e.activation(out=Pb[:, p2, hh, c0:TT], in_=ps[:, 2 + hh, c0:TT], func=AF.Exp, scale=-1.0),
                     reads=["G%d" % hh], writes=["P%dh%d" % (p2, hh)])

        def st_mm2(u, hh):
            if last(u):
                return
            c0 = c0_of(u)
            l3 = u % NL
            Q.op("pe", lambda e: e.matmul(ps[:, 2 + hh, c0:TT], lhsT=SL, rhs=Lb[:, l3, hh, c0:TT], start=False, stop=True,
                                          skip_group_check=True),
                 reads=["L%d" % l3, "cst"], writes=["G%d" % hh], inc=False)

        def st_A(u):
            c0 = c0_of(u)
            p2, e4 = u % 2, u % NE
            if c0 >= 128:
                Q.op("dve", lambda e: e.tensor_tensor(out=Ab[:, p2, :, c0:TT], in0=Eb[:, e4, :, c0:TT], in1=Pb[:, p2, :, c0:TT], op=ALU.mult),
                     reads=["E%d" % e4, "P%dh0" % p2, "P%dh1" % p2], writes=["A%dh0" % p2, "A%dh1" % p2])
            else:
                for hh in range(2):
                    Q.op("dve", lambda e: e.tensor_tensor(out=Ab[:, p2, hh, c0:TT], in0=Eb[:, e4, hh, c0:TT], in1=Pb[:, p2, hh, c0:TT],
                                                          op=ALU.mult),
                         reads=["E%d" % e4, "P%dh%d" % (p2, hh)], writes=["A%dh%d" % (p2, hh)])

        def st_AV(u, step):
            jh, qt, kb = units[u]
            buf = jh % 2
            c0 = c0_of(u)
            p2 = u % 2
            ob = 4
            for hh in range(2):
                hs = slice(hh * 64, (hh + 1) * 64)
                Q.op("pe", lambda e: e.matmul(ps[hs, ob, c0:TT], lhsT=Vt[:, buf, kb, hs], rhs=Ab[:, p2, hh, c0:TT], start=first(u), stop=True,
                                              skip_group_check=True),
                     reads=["Vt%d" % buf, "A%dh%d" % (p2, hh)], writes=["O%d" % ob], inc=(hh == 1))
            if last(u):
                r = chain_i[0] % 2
                tl = slice(qt * TT, (qt + 1) * TT)
                Q.op("dve", lambda e: e.tensor_tensor(out=ozb[:, r, :], in0=ps[:, ob, :], in1=zg[:, buf, tl], op=ALU.mult),
                     reads=["O%d" % ob, "zg%d" % buf], writes=["oz%d" % r])
                new = [[step, it_] for it_ in outproj_items(jh, qt, r)]
                if jh == 7 and not debug:
                    new.append([step, final_item(qt, [e_[1] for e_ in new])])
                if qt == 3 and jh + 2 < 8:
                    new.append([step, Item([lambda: wload(wB_d, wBo_d, jh + 2, jh % 2)], cost=0.0)])
                    new.append([step + 10, pair_prep_item(jh + 2)])
                    new += [[step + 10, it_] for it_ in proj_items(jh + 2)]
                k = 0
                while k < len(bg) and bg[k][0] <= step and bg[k][1].is_out:
                    k += 1
                bg[k:k] = new
                chain_i[0] += 1

        items0 = proj_items(0)
        bg.append([-100, pair_prep_item(0)])
        bg += [[-100, it_] for it_ in items0]
        if debug:
            bg_flush_all()

        def ensure(items, step):
            guard = 0
            while not all(i_.done for i_ in items):
                bg_advance()
                bg_start(step, 100.0, force=True)
                guard += 1
                assert guard < 1000
        dbg_dump("B0", [kT[:, 0, :].bitcast(F32), qT[:, 0, :].bitcast(F32), zg[:, 0, :].bitcast(F32),
                        Vt[:, 0, :, :].rearrange("p a b -> p (a b)").bitcast(F32)],
                 [out_d[:, 0, 0:1024], out_d[:, 1, 0:1024], out_d[:, 2, 0:1024], out_d[:, 3, 0:1024]], [])
        bg.append([6, pair_prep_item(1)])
        bg += [[6, it_] for it_ in proj_items(1)]

        for s in range(-4, n):
            if s + 4 < n and units[s + 4][1] == 0 and units[s + 4][2] == 3 and units[s + 4][0] > 0:
                bg_flush(s)
            if 0 <= s < n:
                st_P(s)
            if 0 <= s + 3 < n:
                st_E(s + 3)
            if 0 <= s + 2 < n:
                st_L(s + 2)
            if 0 <= s < n:
                st_A(s)
            if 0 <= s + 3 < n:
                st_M(s + 3)
            for hh in range(2):
                if 0 <= s < n:
                    st_mm2(s, hh)
                if 0 <= s + 1 < n:
                    st_mm1(s + 1, hh)
            if 0 <= s + 4 < n:
                if units[s + 4][0] == 0 and not debug:
                    ensure([items0[4 * (units[s + 4][2] // 4)], items0[4 * units[s + 4][1] + 2]], s)
                st_QK(s + 4)
            if 0 <= s < n:
                if units[s][0] == 0 and not debug:
                    need = [items0[4 * (units[s][2] // 4) + 1]]
                    if last(s):
                        need.append(items0[4 * units[s][1] + 3])
                    ensure(need, s)
                st_AV(s, s)
            bg_advance()
            u_ref = min(max(s + 2, 0), n - 1)
            bg_start(s, 1.3 if c0_of(u_ref) == 0 else 0.3)
        bg_flush_all()
        if debug == "B1":
            dbg_dump("B1", xT[:], out_d, [])

        dbg_dump("B", xT[:], out_d, [("x", c, t) for c in range(8) for t in range(4)])
        Q.wait_all("sp", keys=["dma_out"])
        Q.emit(nc, st)
    except _Stop:
        pass
    return nc


_DEBUG = False
ZPOOL = False
FINAL_BARRIER = False


def _chunkT(v):
    v = np.asarray(v, np.float32)
    lead = v.shape[:-1]
    return np.ascontiguousarray(np.moveaxis(v.reshape(lead + (8, 128)), -1, 0))


def _prep_shared(a_mod_w, a_mod_b, a_norm_g, a_w_in, a_conv_w, a_w_out, kv_mod_w, kv_mod_b, kv_norm_g, w_kv,
                 b_mod_w, b_mod_b, b_norm_g, b_w_qz, b_w_out, final_norm_g):
    f = lambda a: np.asarray(a, np.float32)
    modW = np.concatenate([f(a_mod_w)[0], f(kv_mod_w), f(b_mod_w)[0]], axis=1)
    modT = np.ascontiguousarray(modW.T.reshape(64, 128, D)[0:24])
    w2 = modW[:, 3072:].reshape(8, 128, 10, 512)
    modW2 = np.ascontiguousarray(w2.transpose(2, 1, 0, 3))
    modb = np.concatenate([f(a_mod_b)[0], f(kv_mod_b), f(b_mod_b)[0]])
    vecs = np.zeros((128, 120), np.float32)
    vecs[:, 0:64] = modb.reshape(64, 128).T
    g4 = np.stack([f(a_norm_g)[0], f(kv_norm_g), f(b_norm_g)[0], f(final_norm_g)])
    vecs[:, 64:96] = g4.reshape(4, 8, 128).transpose(2, 0, 1).reshape(128, 32)
    cw = f(a_conv_w)[0]
    vecs[:, 96:120] = cw.reshape(3, 8, 128).transpose(2, 0, 1).reshape(128, 24)
    win = f(a_w_in)[0].reshape(8, 128, 4, 8, 128)
    wA = np.ascontiguousarray(win.transpose(3, 1, 0, 2, 4).reshape(8, 128, 8, 512))
    wAo = np.ascontiguousarray(f(a_w_out)[0].reshape(8, 128, D))
    wkv = f(w_kv).reshape(8, 128, 2, 8, 128)
    wqz = f(b_w_qz)[0].reshape(8, 128, 2, 8, 128)
    wcat = np.concatenate([wkv, wqz], axis=2)
    wB = np.ascontiguousarray(wcat.transpose(3, 1, 0, 2, 4).reshape(8, 128, 8, 512))
    wBo = np.ascontiguousarray(f(b_w_out)[0].reshape(8, 128, D))
    return dict(modT=modT, modW2=modW2, vecs=vecs, wA=wA, wAo=wAo, wB=wB, wBo=wBo)


def kernel(x, c, a_mod_w, a_mod_b, a_norm_g, a_w_in, a_conv_w, a_w_out, kv_mod_w, kv_mod_b, kv_norm_g, w_kv,
           b_mod_w, b_mod_b, b_norm_g, b_w_qz, b_w_out, final_norm_g):
    x = np.asarray(x, np.float32)
    c = np.asarray(c, np.float32)
    shared = _prep_shared(a_mod_w, a_mod_b, a_norm_g, a_w_in, a_conv_w, a_w_out, kv_mod_w, kv_mod_b, kv_norm_g, w_kv,
                          b_mod_w, b_mod_b, b_norm_g, b_w_qz, b_w_out, final_norm_g)
    in_maps = []
    for b in range(NB):
        m = dict(shared)
        m["xT"] = np.ascontiguousarray(x[b].T.reshape(8, 128, S).transpose(1, 0, 2))
        m["crep"] = np.ascontiguousarray(np.broadcast_to(c[b][None, :], (128, D)))
        m["cT"] = np.ascontiguousarray(c[b].reshape(8, 128).T)
        in_maps.append(m)
    nc = build_nc(debug=_DEBUG)
    res = run_bass_kernel_spmd(nc, in_maps, core_ids=list(range(NB)))
    out = np.empty((NB, S, D), np.float32)
    for b in range(NB):
        oT = np.asarray(res.results[b]["outT"], np.float32)
        out[b] = oT.transpose(2, 1, 0).reshape(S, D)
    return out
```
